# Optimizing a Trainium2 kernel written in Bass

```python
import math
import jax, jax.numpy as jnp
from jax import lax
import numpy as np

D_MODEL = 1024
BATCH = 4
SEQ = 4096
DEPTH = 4

HEAD_DIM = 64
N_ATT_HEADS = 8
ATT_WIDTH = N_ATT_HEADS * HEAD_DIM
N_CONV_GROUPS = 8
CONV_WIDTH = 512
MIX_WIDTH = ATT_WIDTH + CONV_WIDTH
IN_COLS = 4 * ATT_WIDTH + 4 * CONV_WIDTH
CONV_K = 3
DILATED_CONFIGS = ((128, 1), (512, 4), (2048, 16))
NUM_BUCKETS = 32
MAX_DISTANCE = 1024
Q_BLOCK = 128
EPS = 1e-6

kernel_name = "hymba_dilated_attn_shortconv_encoder"


def rms_norm(x, g):
    xf = x.astype(jnp.float32)
    y = xf * lax.rsqrt(jnp.mean(xf * xf, axis=-1, keepdims=True) + EPS)
    return (y * g.astype(jnp.float32)).astype(x.dtype)


def dilated_offsets(window, dilation):
    half = window // (2 * dilation)
    return jnp.arange(-half, half + 1, dtype=jnp.int32) * dilation


def t5_bucket(rel):
    nb = NUM_BUCKETS // 2
    max_exact = nb // 2
    ret = jnp.where(rel > 0, nb, 0)
    n = jnp.abs(rel)
    nf = jnp.maximum(n, 1).astype(jnp.float32)
    large = max_exact + (jnp.log(nf / max_exact) / math.log(MAX_DISTANCE / max_exact)
                         * (nb - max_exact)).astype(jnp.int32)
    large = jnp.minimum(large, nb - 1)
    return ret + jnp.where(n < max_exact, n, large)


def dilated_attention(q, k, v, rel_bias):
    B, S, H, hd = q.shape
    qf = (q.astype(jnp.float32) * (hd ** -0.5)).transpose(0, 2, 1, 3)
    kf = k.astype(jnp.float32).transpose(0, 2, 1, 3)
    vf = v.astype(jnp.float32).transpose(0, 2, 1, 3)
    offsets = [dilated_offsets(w, d) for (w, d) in DILATED_CONFIGS]
    biases = [rel_bias.astype(jnp.float32)[t5_bucket(o)].T for o in offsets]

    def block(n):
        start = n * Q_BLOCK
        qb = lax.dynamic_slice_in_dim(qf, start, Q_BLOCK, axis=2)
        pos = start + jnp.arange(Q_BLOCK, dtype=jnp.int32)
        lses, outs = [], []
        for off, bias in zip(offsets, biases):
            idx = pos[:, None] + off[None, :]
            valid = (idx >= 0) & (idx < S)
            idx = jnp.clip(idx, 0, S - 1)
            kg = jnp.take(kf, idx, axis=2)
            vg = jnp.take(vf, idx, axis=2)
            s = jnp.einsum('bhqd,bhqmd->bhqm', qb, kg) + bias[None, :, None, :]
            s = jnp.where(valid[None, None], s, -jnp.inf)
            lse = jax.nn.logsumexp(s, axis=-1)
            p = jnp.exp(s - lse[..., None])
            outs.append(jnp.einsum('bhqm,bhqmd->bhqd', p, vg))
            lses.append(lse)
        w = jax.nn.softmax(jnp.stack(lses, axis=0), axis=0)
        return jnp.sum(w[..., None] * jnp.stack(outs, axis=0), axis=0)

    o = lax.map(block, jnp.arange(S // Q_BLOCK, dtype=jnp.int32))
    return o.transpose(1, 0, 3, 2, 4).reshape(B, S, H * hd)


def short_conv(z, w):
    zp = jnp.pad(z, ((0, 0), (1, 1), (0, 0)))
    return zp[:, :-2] * w[0] + zp[:, 1:-1] * w[1] + zp[:, 2:] * w[2]


def setup_inputs(seed: int = 0) -> dict:
    key = jax.random.key(seed)
    ks = jax.random.split(key, 10)
    f32 = jnp.float32
    x = jax.random.normal(ks[0], (BATCH, SEQ, D_MODEL), f32)
    c = jax.random.normal(ks[1], (BATCH, D_MODEL), f32)
    w_ada = jax.random.normal(ks[2], (DEPTH, D_MODEL, 3 * D_MODEL), f32) * (0.5 * D_MODEL ** -0.5)
    b_ada = jax.random.normal(ks[3], (DEPTH, 3 * D_MODEL), f32) * 0.02
    pre_norm_g = 1.0 + 0.05 * jax.random.normal(ks[4], (DEPTH, D_MODEL), f32)
    w_in = jax.random.normal(ks[5], (DEPTH, D_MODEL, IN_COLS), f32) * (D_MODEL ** -0.5)
    conv_w = jax.random.normal(ks[6], (DEPTH, CONV_K, CONV_WIDTH), f32) * (CONV_K ** -0.5)
    rel_bias = jax.random.normal(ks[7], (NUM_BUCKETS, N_ATT_HEADS), f32) * 0.5
    w_out = jax.random.normal(ks[8], (DEPTH, MIX_WIDTH, D_MODEL), f32) * (MIX_WIDTH ** -0.5)
    post_norm_g = 1.0 + 0.05 * jax.random.normal(ks[9], (DEPTH, D_MODEL), f32)
    return {"x": x, "c": c, "w_ada": w_ada, "b_ada": b_ada, "pre_norm_g": pre_norm_g,
            "w_in": w_in, "conv_w": conv_w, "rel_bias": rel_bias, "w_out": w_out,
            "post_norm_g": post_norm_g}


def reference(x, c, w_ada, b_ada, pre_norm_g, w_in, conv_w, rel_bias, w_out, post_norm_g):
    B, S, _ = x.shape
    A, Cw = ATT_WIDTH, CONV_WIDTH
    c_act = jax.nn.silu(c)
    for l in range(DEPTH):
        mod = c_act @ w_ada[l] + b_ada[l]
        shift, scale, gate = jnp.split(mod, 3, axis=-1)
        h = rms_norm(x, pre_norm_g[l]) * (1.0 + scale[:, None, :]) + shift[:, None, :]
        proj = h @ w_in[l]
        q = proj[..., 0:A].reshape(B, S, N_ATT_HEADS, HEAD_DIM)
        k = proj[..., A:2 * A].reshape(B, S, N_ATT_HEADS, HEAD_DIM)
        v = proj[..., 2 * A:3 * A].reshape(B, S, N_ATT_HEADS, HEAD_DIM)
        g_att = proj[..., 3 * A:4 * A]
        o0 = 4 * A
        u = proj[..., o0:o0 + Cw]
        b_gate = proj[..., o0 + Cw:o0 + 2 * Cw]
        c_gate = proj[..., o0 + 2 * Cw:o0 + 3 * Cw]
        g_conv = proj[..., o0 + 3 * Cw:o0 + 4 * Cw]
        att = dilated_attention(q, k, v, rel_bias).astype(h.dtype) * jax.nn.silu(g_att)
        cnv = b_gate * short_conv(c_gate * u, conv_w[l]) * jax.nn.silu(g_conv)
        mix = jnp.concatenate([att, cnv], axis=-1) @ w_out[l]
        x = x + gate[:, None, :] * rms_norm(mix, post_norm_g[l])
    return x
```

```python
import math
import numpy as np
import concourse.bass as bass
import concourse.mybir as mybir
from concourse.bass_utils import run_bass_kernel_spmd

F32 = mybir.dt.float32
BF16 = mybir.dt.bfloat16
AF = mybir.ActivationFunctionType
ALU = mybir.AluOpType

P = 128
T = 2048
D = 1024
NFT = 8
TT = 512
NTT = 4
INC = 4096
EPS = 1e-6
NEG = -30000.0
NCORES = 8
DILS = (1, 4, 16)


class Sched:
    def __init__(self, nc):
        self.nc = nc
        self.ops = []
        self.lastw = {}
        self.readers = {}

    def add(self, eng, fn, reads=(), writes=(), dma=None, inc=16, sync_same=False, batch=None):
        idx = len(self.ops)
        deps = set()
        for r in reads:
            if r in self.lastw:
                deps.add(self.lastw[r])
        for w in writes:
            if w in self.lastw:
                deps.add(self.lastw[w])
            for rd in self.readers.get(w, ()):
                deps.add(rd)
        deps.discard(idx)
        for r in reads:
            self.readers.setdefault(r, []).append(idx)
        for w in writes:
            self.lastw[w] = idx
            self.readers[w] = []
        self.ops.append(dict(eng=eng, fn=fn, deps=deps, dma=dma, inc=inc, sync_same=sync_same, batch=batch))
        return idx

    def emit(self, final_waits=()):
        nc = self.nc
        ops = self.ops
        need = [False] * len(ops)
        for i, o in enumerate(ops):
            for d in o["deps"]:
                od = ops[d]
                if od["dma"] is not None:
                    continue
                if od["eng"] != o["eng"] or o["dma"] is not None or o["sync_same"] or o["eng"] != "pe":
                    need[d] = True
        for d in final_waits:
            if ops[d]["dma"] is None:
                need[d] = True
        engs = ["pe", "act", "dve", "pool", "sp"]
        esem = {e: nc.alloc_semaphore(name=f"sem_{e}") for e in engs}
        ssem = {}
        ecnt = {e: 0 for e in engs}
        scnt = {}
        token = [None] * len(ops)
        bfinal = {}
        for i, o in enumerate(ops):
            if o["dma"] is not None:
                s = o["dma"]
                if s not in ssem:
                    ssem[s] = nc.alloc_semaphore(name=f"sd_{len(ssem)}")
                    scnt[s] = 0
                scnt[s] += o["inc"]
                token[i] = (ssem[s], scnt[s], ("s", s))
                if o["batch"] is not None:
                    bfinal[(s, o["batch"])] = scnt[s]
            elif need[i]:
                ecnt[o["eng"]] += 1
                token[i] = (esem[o["eng"]], ecnt[o["eng"]], ("e", o["eng"]))
        for i, o in enumerate(ops):
            if o["dma"] is not None and o["batch"] is not None:
                sem, val, key = token[i]
                token[i] = (sem, bfinal[(o["dma"], o["batch"])], key)
        self.nsem = len(ssem) + len(engs)

        def emit_engine(e, handle):
            waited = {}
            for i, o in enumerate(ops):
                if o["eng"] != e:
                    continue
                wants = {}
                for d in sorted(o["deps"]):
                    od = ops[d]
                    if od["dma"] is None and od["eng"] == e and o["dma"] is None and not o["sync_same"] and e == "pe":
                        continue
                    sem, val, key = token[d]
                    if key not in wants or wants[key][1] < val:
                        wants[key] = (sem, val)
                for key, (sem, val) in wants.items():
                    if waited.get(key, 0) >= val:
                        continue
                    handle.wait_ge(sem, val)
                    waited[key] = val
                ins = o["fn"](handle)
                if token[i] is not None:
                    sem, val, key = token[i]
                    ins.then_inc(sem, o["inc"] if o["dma"] is not None else 1)
            if e == "sp":
                for d in final_waits:
                    sem, val, key = token[d]
                    if waited.get(key, 0) >= val:
                        continue
                    handle.wait_ge(sem, val)
                    waited[key] = val

        with nc.Block() as block:
            @block.tensor
            def _(h):
                emit_engine("pe", h)

            @block.scalar
            def _(h):
                emit_engine("act", h)

            @block.vector
            def _(h):
                emit_engine("dve", h)

            @block.gpsimd
            def _(h):
                emit_engine("pool", h)

            @block.sync
            def _(h):
                emit_engine("sp", h)


def t5_bucket_np(rel):
    rel = np.asarray(rel, np.int64)
    nb = 16
    me = 8
    ret = np.where(rel > 0, nb, 0)
    n = np.abs(rel)
    nf = np.maximum(n, 1).astype(np.float32)
    large = me + (np.log(nf / np.float32(me)) / np.float32(math.log(1024 / me))
                  * np.float32(nb - me)).astype(np.int32)
    large = np.minimum(large, nb - 1)
    return ret + np.where(n < me, n, large)


def make_onehot():
    oh = np.zeros((33, 3 * 384), np.float32)
    for ci, d in enumerate(DILS):
        for u in range(384):
            m = u - 191
            if abs(m) <= 64:
                oh[int(t5_bucket_np(m * d)), ci * 384 + u] = 1.0
            else:
                oh[32, ci * 384 + u] = 1.0
    return oh


class _StopBuild(Exception):
    pass


STOP = None
DBG_SKIP_V = False
DBG_SKIP_PV = False
DBG_SKIP_UNITS = False
DBG_SKIP_QPERM = False


def build_program(NL):
    nc = bass.Bass("TRN2", target_bir_lowering=False)
    S = Sched(nc)

    def checkpoint(name):
        if STOP == name:
            raise _StopBuild()

    def dram(name, shape, dt, kind="ExternalInput"):
        return nc.dram_tensor(name, list(shape), dt, kind=kind)

    xT_d = dram("xT", [D, T], F32)
    cT_d = dram("cT", [P, 8], F32)
    wada_d = dram("w_ada", [NL, D, 3 * D], F32)
    bada_d = dram("b_adaT", [P, NL, 24], F32)
    gpre_d = dram("gpreT", [P, NL, 8], F32)
    gpost_d = dram("gpostT", [P, NL, 8], F32)
    win_d = dram("w_in", [NL, D, INC], F32)
    wout_d = dram("w_out", [NL, D, D], F32)
    cw_d = dram("cwT", [P, NL, 3, 4], F32)
    relb_d = dram("rel_bias", [32, 8], F32)
    oh_d = dram("onehot", [33, 1152], F32)
    flags_d = dram("flags", [P, 4], F32)
    yT_d = dram("yT", [D, T], F32, kind="ExternalOutput")
    pub_d = dram("pub", [1024, 2048], BF16, kind="Internal")
    gath_d = dram("gath", [2048, 2048], BF16, kind="Internal")
    pub2_d = dram("pub2", [P, 16], F32, kind="Internal")
    gath2_d = dram("gath2", [2 * P, 16], F32, kind="Internal")
    E_d = dram("Evec", [8, 1152], F32, kind="Internal")
    groups = [[0, 1], [2, 3], [4, 5], [6, 7]]

    sb = nc.alloc_sbuf_tensor
    xT = sb("xTs", [P, NFT, T], F32)
    R1 = sb("R1", [P, 16384], BF16)
    mix = sb("mix", [P, NFT, T], BF16)
    Qnat = sb("Qnat", [P, 4, T], BF16)
    Wt = sb("Wt", [P, 3, 4, 512], BF16)
    wring = sb("wring", [P, 2, 8, 256], BF16)
    Vring = sb("Vring", [P, 3, 8, 128], BF16)
    Pex = sb("Pex", [P, 2, 512], BF16)
    Pw = sb("Pw", [P, 2, 512], BF16)
    BIG = sb("BIG", [P, 4096], F32)
    TMP = sb("TMP", [P, 4, 512], F32)
    sq = sb("sq", [P, 2, 512], BF16)
    rstdT = sb("rstdT", [P, 1, 512], F32)
    stg = sb("stg", [P, 3, 512], BF16)
    ones_bf = sb("ones_bf", [P, P], BF16)
    onesL = sb("onesL", [P, 64], BF16)
    onesR = sb("onesR", [P, 64], BF16)
    epsT = sb("epsT", [P, 1], F32)
    flags = sb("flagsS", [P, 4], F32)
    cact = sb("cact", [P, 8], BF16)
    cTs = sb("cTs", [P, 8], F32)
    modT = sb("modT", [P, NL, 24], F32)
    badaT = sb("badaT", [P, NL, 24], F32)
    gpreT = sb("gpreTs", [P, NL, 8], F32)
    gpostT = sb("gpostTs", [P, NL, 8], F32)
    gmod = sb("gmod", [P, NL, 8], F32)
    ggate = sb("ggate", [P, NL, 8], F32)
    cwT = sb("cwTs", [P, NL, 3, 4], F32)
    cub = sb("cub", [P, 4, 4], F32)
    Bg = sb("Bg", [P, 4, 2], F32)
    cnb = sb("cnb", [P, 2, 16], F32)
    fx = sb("fx", [P, 10, 4], F32)
    relb = sb("relb", [33, 8], F32)

    ps = [nc.alloc_psum_tensor(f"ps{i}", [P, 512], F32) for i in range(8)]

    hT = R1[:].rearrange("p (f t) -> p f t", f=8)

    def r_x(ft, tt): return ("x", ft, tt)
    def r_h(ft, tt): return ("R1", ft, tt)
    def r_r1chunk(c): return [("R1", c, q) for q in range(4)]
    def r_mix(ft, tt): return ("mix", ft, tt)
    def r_q(p, tt): return ("Qn", p, tt)
    def r_ps(i): return ("ps", i)
    def r_big(c): return ("BIG", c)
    def r_tmp(i): return ("TMP", i)

    class Ring:
        def __init__(self, n): self.n = n; self.i = 0
        def next(self):
            v = self.i % self.n; self.i += 1; return v
    tmp_ring = Ring(4)
    rstd_ring = Ring(1)
    sq_ring = Ring(2)
    w_ring = Ring(2)
    stg_ring = Ring(3)
    ev_flip = Ring(2)

    for ft in range(NFT):
        S.add("sp", lambda e, ft=ft: e.dma_start(out=xT[:, ft, :], in_=xT_d.ap()[ft * P:(ft + 1) * P, :]),
              writes=[r_x(ft, tt) for tt in range(NTT)], dma=f"x{ft}")
    small = [(cTs, cT_d, "cTs"), (badaT, bada_d, "bada"), (gpreT, gpre_d, "gpre"), (gpostT, gpost_d, "gpost"),
             (cwT, cw_d, "cw"), (flags, flags_d, "flags")]
    for t_, d_, nm in small:
        S.add("sp", lambda e, t_=t_, d_=d_: e.dma_start(out=t_[:], in_=d_.ap()), writes=[nm], dma="s_" + nm)
    S.add("sp", lambda e: e.dma_start(out=relb[0:32, :], in_=relb_d.ap()), writes=["relb"], dma="s_relb")
    S.add("sp", lambda e: e.dma_start(out=BIG[0:33, 0:1152], in_=oh_d.ap()), writes=[r_big(0), r_big(1), r_big(2)], dma="s_oh")
    S.add("dve", lambda e: e.memset(relb[32:33, :], NEG), writes=["relb32"])
    S.add("dve", lambda e: e.memset(ones_bf[:], 1.0), writes=["ones"])
    S.add("dve", lambda e: e.memset(epsT[:], EPS), writes=["eps"])
    S.add("dve", lambda e: e.tensor_scalar(onesL[:], ones_bf[:, 0:64], flags[:, 2:3], None, ALU.mult),
          reads=["ones", "flags"], writes=["onesL"], sync_same=True)
    S.add("dve", lambda e: e.tensor_scalar(onesR[:], ones_bf[:, 0:64], flags[:, 3:4], None, ALU.mult),
          reads=["ones", "flags"], writes=["onesR"], sync_same=True)
    S.add("act", lambda e: e.activation(cact[:], cTs[:], AF.Silu), reads=["cTs"], writes=["cact"])

    def f_E(e):
        ins = None
        for ci in range(3):
            ins = e.matmul(ps[6][0:8, ci * 128:ci * 128 + 128], relb[:, :], BIG[0:33, ci * 384:ci * 384 + 128], start=True, stop=True)
            ins = e.matmul(ps[7][0:8, ci * 128:ci * 128 + 128], relb[:, :], BIG[0:33, ci * 384 + 128:ci * 384 + 256], start=True, stop=True)
            ins = e.matmul(ps[5][0:8, ci * 128:ci * 128 + 128], relb[:, :], BIG[0:33, ci * 384 + 256:ci * 384 + 384], start=True, stop=True)
        return ins
    S.add("pe", f_E, reads=["relb", "relb32", r_big(0), r_big(1), r_big(2)], writes=[r_ps(5), r_ps(6), r_ps(7)])
    Esb = BIG[0:8, 2048:2048 + 1152].rearrange("p (c u) -> p c u", c=3)

    def f_Ecopy(e):
        ins = None
        for j, pi in enumerate((6, 7, 5)):
            ins = e.tensor_copy(Esb[:, :, j * 128:(j + 1) * 128], ps[pi][0:8, 0:384].rearrange("p (c u) -> p c u", c=3))
        return ins
    S.add("dve", f_Ecopy, reads=[r_ps(5), r_ps(6), r_ps(7)], writes=[r_big(4), r_big(5), r_big(6)])
    S.add("sp", lambda e: e.dma_start(out=E_d.ap(), in_=BIG[0:8, 2048:2048 + 1152]),
          reads=[r_big(4), r_big(5), r_big(6)], writes=["E_d"], dma="s_E")
    for ci in range(3):
        for h in range(8):
            ti = tmp_ring.next()
            p_, hh = h // 2, h % 2
            hank = bass.AP(E_d, h * 1152 + ci * 384, [[1, 128], [1, 256]])
            tv = TMP[:, ti, 0:256]
            S.add("sp", lambda e, tv=tv, hank=hank: e.dma_start(out=tv, in_=hank),
                  reads=["E_d"], writes=[r_tmp(ti)], dma=f"tmp{ti}")

            def f_W(e, ci=ci, p_=p_, hh=hh, ti=ti):
                e.activation(Wt[:, ci, p_, hh * 256:hh * 256 + 128], TMP[:, ti, 127::-1], AF.Exp)
                return e.activation(Wt[:, ci, p_, hh * 256 + 128:hh * 256 + 256], TMP[:, ti, 255:127:-1], AF.Exp)
            S.add("act", f_W, reads=[r_tmp(ti)], writes=[("Wt", ci, p_, hh)])

    def load_w(src_ap_kmajor, ncols):
        sl = w_ring.next()
        S.add("pool", lambda e: e.dma_start(out=wring[:, sl, :, 0:ncols],
                                            in_=src_ap_kmajor.rearrange("(ko ki) c -> ki ko c", ki=P)),
              writes=[("wr", sl)], dma=f"wr{sl}")
        return sl, ("wr", sl)

    def ada_tasks(l):
        tasks = []
        slots = {}

        def t_load(blk):
            def f():
                slots[blk] = load_w(wada_d.ap()[l, :, blk * 256:(blk + 1) * 256], 256)
            return f

        def t_mm(blk):
            def f():
                sl, rw = slots[blk]

                def f_mod(e):
                    ins = None
                    for j in range(2):
                        col = l * 24 + blk * 2 + j
                        for ko in range(8):
                            ins = e.matmul(ps[4][:, col:col + 1], wring[:, sl, ko, j * 128:(j + 1) * 128], cact[:, ko:ko + 1],
                                           start=(ko == 0), stop=(ko == 7))
                    return ins
                S.add("pe", f_mod, reads=[rw, "cact"], writes=[r_ps(4)])
            return f

        def t_fin():
            S.add("dve", lambda e: e.tensor_tensor(modT[:, l, :], ps[4][:, l * 24:(l + 1) * 24], badaT[:, l, :], ALU.add),
                  reads=[r_ps(4), "bada"], writes=[("modT", l)])
            S.add("dve", lambda e: e.scalar_tensor_tensor(gmod[:, l, :], modT[:, l, 8:16], 1.0, gpreT[:, l, :], ALU.add, ALU.mult),
                  reads=[("modT", l), "gpre"], writes=[("gmod", l)], sync_same=True)
            S.add("dve", lambda e: e.tensor_tensor(ggate[:, l, :], modT[:, l, 16:24], gpostT[:, l, :], ALU.mult),
                  reads=[("modT", l), "gpost"], writes=[("ggate", l)], sync_same=True)
        for blk in range(12):
            tasks.append(t_load(blk))
            if blk >= 1:
                tasks.append(t_mm(blk - 1))
        tasks.append(t_mm(11))
        tasks.append(t_fin)
        return tasks

    for t_ in ada_tasks(0):
        t_()

    def rms_stats(src_fn, src_res_fn, tt, ss_bank):
        for ft in range(NFT):
            si = sq_ring.next()
            S.add("act", lambda e, ft=ft, si=si: e.activation(sq[:, si, :], src_fn(ft), AF.Square),
                  reads=[src_res_fn(ft)], writes=[("sq", si)])
            S.add("pe", lambda e, ft=ft, si=si: e.matmul(ps[ss_bank][:, :], ones_bf[:, :], sq[:, si, :],
                                                         start=(ft == 0), stop=(ft == 7)),
                  reads=[("sq", si), "ones"], writes=[r_ps(ss_bank)])
        ti = rstd_ring.next()

        S.add("act", lambda e, ti=ti: e.activation(rstdT[:, ti, :], ps[ss_bank][:, :], AF.Ln, bias=epsT[:, 0:1], scale=1.0 / D),
              reads=[r_ps(ss_bank), "eps"], writes=[("rstd", ti)])
        S.add("act", lambda e, ti=ti: e.activation(rstdT[:, ti, :], rstdT[:, ti, :], AF.Exp, scale=-0.5),
              reads=[("rstd", ti)], writes=[("rstd", ti)])
        return ti

    bank_ring = Ring(4)

    out_dmas = []

    def layer_body(l):
        if STOP == "Eout":
            S.add("sp", lambda e: e.dma_start(out=BIG[0:8, 0:1152], in_=E_d.ap()), reads=["E_d"], writes=[r_big(0), r_big(1), r_big(2)], dma="dbgE")
            S.add("dve", lambda e: e.tensor_copy(xT[0:8, 0, 0:1152], BIG[0:8, 0:1152]), reads=[r_big(0), r_big(1), r_big(2)], writes=[r_x(0, 0), r_x(0, 1), r_x(0, 2)])
            for p2 in range(4):
                for ci2 in range(3):
                    S.add("dve", lambda e, p2=p2, ci2=ci2: e.tensor_copy(xT[:, 1 + p2, ci2 * 512:(ci2 + 1) * 512], Wt[:, ci2, p2, :]),
                          reads=[("Wt", ci2, p2, 0), ("Wt", ci2, p2, 1)], writes=[r_x(1 + p2, ci2)])
            raise _StopBuild()
        checkpoint("setup")
        for tt in range(NTT):
            tsl = slice(tt * TT, (tt + 1) * TT)
            ri = rms_stats(lambda ft, tsl=tsl: xT[:, ft, tsl], lambda ft, tt=tt: r_x(ft, tt), tt, 6)
            for ft in range(NFT):
                ti = tmp_ring.next()
                S.add("dve", lambda e, ft=ft, tsl=tsl, ti=ti, ri=ri: e.tensor_tensor(TMP[:, ti, :], xT[:, ft, tsl], rstdT[:, ri, :], ALU.mult),
                      reads=[r_x(ft, tt), ("rstd", ri)], writes=[r_tmp(ti)])
                S.add("act", lambda e, ft=ft, tsl=tsl, ti=ti, l=l: e.activation(hT[:, ft, tsl], TMP[:, ti, :], AF.Identity,
                                                                               bias=modT[:, l, ft:ft + 1], scale=gmod[:, l, ft:ft + 1]),
                      reads=[r_tmp(ti), ("gmod", l), ("modT", l)], writes=[r_h(ft, tt)])

        checkpoint("prenorm")
        def proj_tile(l, sl, rw, j, tt, bank):
            def f(e):
                ins = None
                for ko in range(8):
                    ins = e.matmul(ps[bank][:, :], wring[:, sl, ko, j * 128:(j + 1) * 128], hT[:, ko, tt * TT:(tt + 1) * TT],
                                   start=(ko == 0), stop=(ko == 7))
                return ins
            S.add("pe", f, reads=[rw] + [r_h(ko, tt) for ko in range(8)], writes=[r_ps(bank)])

        win = win_d.ap()
        for blk in range(2):
            sl, rw = load_w(win[l, :, 512 + blk * 256:512 + (blk + 1) * 256], 256)
            for j in range(2):
                p_ = blk * 2 + j
                for tt in range(NTT):
                    bank = bank_ring.next()
                    proj_tile(l, sl, rw, j, tt, bank)
                    si = stg_ring.next()
                    eng = "act" if ev_flip.next() == 0 else "dve"
                    if eng == "act":
                        S.add("act", lambda e, bank=bank, si=si: e.activation(stg[:, si, :], ps[bank][:, :], AF.Copy),
                              reads=[r_ps(bank)], writes=[("stg", si)])
                    else:
                        S.add("dve", lambda e, bank=bank, si=si: e.tensor_copy(stg[:, si, :], ps[bank][:, :]),
                              reads=[r_ps(bank)], writes=[("stg", si)])
                    S.add("sp", lambda e, si=si, p_=p_, tt=tt: e.dma_start(out=pub_d.ap()[p_ * P:(p_ + 1) * P, tt * TT:(tt + 1) * TT], in_=stg[:, si, :]),
                          reads=[("stg", si)], writes=[("pubK", p_, tt)], dma=f"stg{si}")
        for blk in range(2):
            sl, rw = load_w(win[l, :, 1024 + blk * 256:1024 + (blk + 1) * 256], 256)
            for tk in range(16):
                bank = bank_ring.next()

                def f_v(e, sl=sl, tk=tk, bank=bank):
                    ins = None
                    for ko in range(8):
                        ins = e.matmul(ps[bank][:, 0:256], hT[:, ko, tk * P:(tk + 1) * P], wring[:, sl, ko, :],
                                       start=(ko == 0), stop=(ko == 7))
                    return ins
                S.add("pe", f_v, reads=[rw] + [r_h(ko, tk // 4) for ko in range(8)], writes=[r_ps(bank)])
                si = stg_ring.next()
                eng = "act" if ev_flip.next() == 0 else "dve"
                if eng == "act":
                    S.add("act", lambda e, bank=bank, si=si: e.activation(stg[:, si, 0:256], ps[bank][:, 0:256], AF.Copy),
                          reads=[r_ps(bank)], writes=[("stg", si)])
                else:
                    S.add("dve", lambda e, bank=bank, si=si: e.tensor_copy(stg[:, si, 0:256], ps[bank][:, 0:256]),
                          reads=[r_ps(bank)], writes=[("stg", si)])
                vdst = bass.AP(pub_d, 512 * 2048 + tk * P * 512 + blk * 256, [[512, P], [1, 256]])
                S.add("sp", lambda e, si=si, vdst=vdst: e.dma_start(out=vdst, in_=stg[:, si, 0:256]),
                      reads=[("stg", si)], writes=[("pubV", tk, blk)], dma=f"stg{si}")
        checkpoint("kv")
        pending_cc = []
        for k in range(8):
            if k < 4:
                rr = [("pubK", k, tt) for tt in range(4)]
            else:
                rr = [("pubV", tk, b_) for tk in range(4 * (k - 4), 4 * (k - 4) + 4) for b_ in range(2)]

            def cc(k=k, rr=rr):
                S.add("pool", lambda e: e.collective_compute("AllGather", ALU.bypass, replica_groups=groups,
                                                             ins=[pub_d.ap()[k * P:(k + 1) * P, :]],
                                                             outs=[gath_d.ap()[k * 2 * P:(k + 1) * 2 * P, :]]),
                      reads=rr, writes=[("gath", k), "ccchain"], dma="cc1", inc=1)
            pending_cc.append(cc)

        def load_w2(src, ncols):
            r_ = load_w(src, ncols)
            if pending_cc:
                pending_cc.pop(0)()
            return r_

        checkpoint("cc1")
        for blk in range(2):
            sl, rw = load_w2(win[l, :, blk * 256:(blk + 1) * 256], 256)
            for j in range(2):
                p_ = blk * 2 + j
                for tt in range(NTT):
                    bank = bank_ring.next()
                    proj_tile(l, sl, rw, j, tt, bank)
                    if ev_flip.next() == 0:
                        S.add("act", lambda e, bank=bank, p_=p_, tt=tt: e.activation(Qnat[:, p_, tt * TT:(tt + 1) * TT], ps[bank][:, :], AF.Copy),
                              reads=[r_ps(bank)], writes=[r_q(p_, tt)])
                    else:
                        S.add("dve", lambda e, bank=bank, p_=p_, tt=tt: e.tensor_copy(Qnat[:, p_, tt * TT:(tt + 1) * TT], ps[bank][:, :]),
                              reads=[r_ps(bank)], writes=[r_q(p_, tt)])
        for blk in range(2):
            sl, rw = load_w2(win[l, :, 1536 + blk * 256:1536 + (blk + 1) * 256], 256)
            for j in range(2):
                p_ = blk * 2 + j
                for tt in range(NTT):
                    bank = bank_ring.next()
                    proj_tile(l, sl, rw, j, tt, bank)
                    S.add("act", lambda e, bank=bank, p_=p_, tt=tt: e.activation(mix[:, p_, tt * TT:(tt + 1) * TT], ps[bank][:, :], AF.Silu),
                          reads=[r_ps(bank)], writes=[r_mix(p_, tt)])
        checkpoint("qg")
        u_sb = BIG[:, 0:2048]
        cu = BIG[:, 2048:4096]
        for f in range(4):
            sl, rw = load_w2(win[l, :, 2048 + f * 128:2048 + (f + 1) * 128], 128)
            for tt in range(NTT):
                bank = bank_ring.next()
                proj_tile(l, sl, rw, 0, tt, bank)
                S.add("act", lambda e, bank=bank, tt=tt: e.activation(u_sb[:, tt * TT:(tt + 1) * TT], ps[bank][:, :], AF.Copy),
                      reads=[r_ps(bank)], writes=[r_big(tt)])
            sl, rw = load_w2(win[l, :, 3072 + f * 128:3072 + (f + 1) * 128], 128)
            for tt in range(NTT):
                bank = bank_ring.next()
                proj_tile(l, sl, rw, 0, tt, bank)
                S.add("dve", lambda e, bank=bank, tt=tt: e.tensor_tensor(cu[:, tt * TT:(tt + 1) * TT], ps[bank][:, :], u_sb[:, tt * TT:(tt + 1) * TT], ALU.mult),
                      reads=[r_ps(bank), r_big(tt)], writes=[r_big(4 + tt)])
            def f_cub(e, f=f):
                e.tensor_copy(cub[:, f, 0:2], cu[:, 0:2])
                return e.tensor_copy(cub[:, f, 2:4], cu[:, 2046:2048])
            S.add("dve", f_cub, reads=[r_big(4), r_big(7)], writes=[("cub", f)], sync_same=True)
            slB, rwB = load_w2(win[l, :, 2560 + f * 128:2560 + (f + 1) * 128], 128)
            bankB = []
            for tt in range(NTT):
                bank = bank_ring.next()
                proj_tile(l, slB, rwB, 0, tt, bank)
                t1 = tmp_ring.next()

                c0 = tt * TT
                acc = TMP[:, t1, :]
                cu_res = [r_big(4 + t_) for t_ in range(max(0, tt - 1), min(NTT, tt + 2))]
                S.add("dve", lambda e, acc=acc, c0=c0, l=l, f=f: e.tensor_scalar(acc, cu[:, c0:c0 + TT], cwT[:, l, 1, f:f + 1], None, ALU.mult),
                      reads=cu_res + ["cw"], writes=[r_tmp(t1)])
                if tt == 0:
                    S.add("dve", lambda e, acc=acc, l=l, f=f: e.scalar_tensor_tensor(acc[:, 1:TT], cu[:, 0:TT - 1], cwT[:, l, 0, f:f + 1], acc[:, 1:TT], ALU.mult, ALU.add),
                          reads=cu_res + ["cw", r_tmp(t1)], writes=[r_tmp(t1)])
                else:
                    S.add("dve", lambda e, acc=acc, c0=c0, l=l, f=f: e.scalar_tensor_tensor(acc, cu[:, c0 - 1:c0 + TT - 1], cwT[:, l, 0, f:f + 1], acc, ALU.mult, ALU.add),
                          reads=cu_res + ["cw", r_tmp(t1)], writes=[r_tmp(t1)])
                if tt == NTT - 1:
                    S.add("dve", lambda e, acc=acc, c0=c0, l=l, f=f: e.scalar_tensor_tensor(acc[:, 0:TT - 1], cu[:, c0 + 1:c0 + TT], cwT[:, l, 2, f:f + 1], acc[:, 0:TT - 1], ALU.mult, ALU.add),
                          reads=cu_res + ["cw", r_tmp(t1)], writes=[r_tmp(t1)])
                else:
                    S.add("dve", lambda e, acc=acc, c0=c0, l=l, f=f: e.scalar_tensor_tensor(acc, cu[:, c0 + 1:c0 + TT + 1], cwT[:, l, 2, f:f + 1], acc, ALU.mult, ALU.add),
                          reads=cu_res + ["cw", r_tmp(t1)], writes=[r_tmp(t1)])

                def f_convB(e, tt=tt, f=f, bank=bank, acc=acc):
                    if tt == 0:
                        e.tensor_copy(Bg[:, f, 0:1], ps[bank][:, 0:1])
                    if tt == NTT - 1:
                        e.tensor_copy(Bg[:, f, 1:2], ps[bank][:, TT - 1:TT])
                    return e.tensor_tensor(acc, ps[bank][:, :], acc, ALU.mult)
                S.add("dve", f_convB, reads=[r_ps(bank), r_tmp(t1)], writes=[r_tmp(t1), ("Bg", f, tt)])
                bankB.append(t1)
            slG, rwG = load_w2(win[l, :, 3584 + f * 128:3584 + (f + 1) * 128], 128)
            for tt in range(NTT):
                bank = bank_ring.next()
                proj_tile(l, slG, rwG, 0, tt, bank)
                t1 = bankB[tt]
                S.add("act", lambda e, bank=bank, tt=tt: e.activation(u_sb[:, tt * TT:(tt + 1) * TT], ps[bank][:, :], AF.Silu),
                      reads=[r_ps(bank), r_big(4 + tt)], writes=[r_big(tt)])

                def f_cm(e, tt=tt, f=f, t1=t1):
                    if tt == 0:
                        e.tensor_tensor(Bg[:, f, 0:1], Bg[:, f, 0:1], u_sb[:, 0:1], ALU.mult)
                    if tt == NTT - 1:
                        e.tensor_tensor(Bg[:, f, 1:2], Bg[:, f, 1:2], u_sb[:, T - 1:T], ALU.mult)
                    return e.tensor_tensor(mix[:, 4 + f, tt * TT:(tt + 1) * TT], TMP[:, t1, :], u_sb[:, tt * TT:(tt + 1) * TT], ALU.mult)
                S.add("dve", f_cm, reads=[r_tmp(t1), r_big(tt), ("Bg", f, tt)], writes=[r_mix(4 + f, tt), ("Bg", f, tt)])
        while pending_cc:
            pending_cc.pop(0)()
        checkpoint("conv")
        S.add("sp", lambda e: e.dma_start(out=pub2_d.ap(), in_=cub[:].rearrange("p f k -> p (f k)")),
              reads=[("cub", f) for f in range(4)], writes=["pub2"], dma="s_pub2")
        S.add("pool", lambda e: e.collective_compute("AllGather", ALU.bypass, replica_groups=groups,
                                                     ins=[pub2_d.ap()], outs=[gath2_d.ap()]),
              reads=["pub2"], writes=["gath2", "ccchain"], dma="cc2", inc=1)
        S.add("sp", lambda e: e.dma_start(out=cnb[:], in_=gath2_d.ap().rearrange("(r p) k -> p r k", r=2)),
              reads=["gath2"], writes=["cnb"], dma="s_cnb")
        conv_fix_layer = l

        checkpoint("cc2")
        Qp = R1[:, 0:8192].rearrange("p (b c t) -> p b c t", b=2, c=2)
        Kp = R1[:, 8192:16384].rearrange("p (b t) -> p b t", b=2)
        accO = BIG[:, 0:2048]
        accD = BIG[:, 2048:4096]
        pub = pub_d.ap()
        gath = gath_d.ap()
        Vpub_base = 512 * 2048

        def vrow_ap(tensor, rank_row0, tok0, tok_step, nrows, ntile, tile_step, p_):
            base = rank_row0 * 2048 + Vpub_base + tok0 * 512 + p_ * 128
            dims = [[tok_step * 512, nrows]]
            if ntile is not None:
                dims.append([tile_step * 512, ntile])
            dims.append([1, 128])
            return bass.AP(tensor, base, dims)

        def vrow_g(r, tok0, tok_step, nrows, ntile, tile_step, p_):
            last = tok0 + tok_step * (nrows - 1) + (tile_step * (ntile - 1) if ntile else 0)
            assert tok0 // 512 == last // 512, (tok0, last)
            base = ((4 + tok0 // 512) * 256 + r * 128) * 2048 + (tok0 % 512) * 512 + p_ * 128
            dims = [[tok_step * 512, nrows]]
            if ntile is not None:
                dims.append([tile_step * 512, ntile])
            dims.append([1, 128])
            return bass.AP(gath_d, base, dims)

        vg_ring = Ring(3)
        vgcnt = [l * 100000]
        pubV_all = [("pubV", tk, b2) for tk in range(16) for b2 in range(2)]
        gathV_all = [("gath", k) for k in range(4, 8)]
        sbank_ring = Ring(3)
        acc_flip = Ring(2)

        for p_ in range(4):
            kb = p_ % 2
            kres = r_r1chunk(4 + 2 * kb) + r_r1chunk(5 + 2 * kb)
            S.add("sp", lambda e, kb=kb, p_=p_: e.dma_start(out=Kp[:, kb, 0:1024], in_=gath[p_ * 2 * P:p_ * 2 * P + P, 1024:2048]),
                  reads=[("gath", p_)], writes=kres[0:2], dma=f"kp{kb}", batch=(l, p_))
            S.add("sp", lambda e, kb=kb, p_=p_: e.dma_start(out=Kp[:, kb, 1024:3072], in_=pub[p_ * P:(p_ + 1) * P, :]),
                  reads=[("pubK", p_, tt) for tt in range(4)], writes=kres[2:6], dma=f"kp{kb}", batch=(l, p_))
            S.add("sp", lambda e, kb=kb, p_=p_: e.dma_start(out=Kp[:, kb, 3072:4096], in_=gath[p_ * 2 * P + P:(p_ + 1) * 2 * P, 0:1024]),
                  reads=[("gath", p_)], writes=kres[6:8], dma=f"kp{kb}", batch=(l, p_))
            qres4 = r_r1chunk(2 * kb)
            qres16 = r_r1chunk(2 * kb + 1)
            if not DBG_SKIP_QPERM:
              S.add("dve", lambda e, kb=kb, p_=p_: e.tensor_copy(Qp[:, kb, 0, :].rearrange("p (b j) -> p b j", b=4),
                                                               Qnat[:, p_, :].rearrange("p (j b) -> p b j", b=4)),
                  reads=[r_q(p_, tt) for tt in range(4)], writes=qres4)
            if not DBG_SKIP_QPERM:
              S.add("act", lambda e, kb=kb, p_=p_: e.activation(Qp[:, kb, 1, :].rearrange("p (b a i) -> p b a i", b=4, a=4),
                                                              Qnat[:, p_, :].rearrange("p (i a b) -> p b a i", a=4, b=4), AF.Copy),
                  reads=[r_q(p_, tt) for tt in range(4)], writes=qres16)

            pending = []

            for ci, d in enumerate(DILS):
                for c in range(4):
                    vg = vg_ring.next()
                    vres = ("vg", vg)
                    tiles = []
                    dmas = []
                    if d == 1:
                        for s in range(5):
                            kt = 4 * c + s
                            kcol = 960 + 128 * kt
                            edge = 'L' if kt == 0 else ('R' if kt == 16 else None)
                            tiles.append((slice(kcol, kcol + 128, 1), edge))
                        lo = 0
                        if c == 0:
                            dmas.append((Vring[0:64, vg, 0, :], vrow_g(0, 1984, 1, 64, None, 0, p_)))
                            dmas.append((Vring[64:128, vg, 0, :], vrow_ap(pub_d, 0, 0, 1, 64, None, 0, p_)))
                            lo = 1
                        hi = 5
                        if c == 3:
                            dmas.append((Vring[0:64, vg, 4, :], vrow_ap(pub_d, 0, 1984, 1, 64, None, 0, p_)))
                            dmas.append((Vring[64:128, vg, 4, :], vrow_g(1, 0, 1, 64, None, 0, p_)))
                            hi = 4
                        kt0 = 4 * c + lo
                        dmas.append((Vring[:, vg, lo:hi, :], vrow_ap(pub_d, 0, 128 * kt0 - 64, 1, 128, hi - lo, 128, p_)))
                        qsrc = lambda qb, c=c, p_=p_: Qnat[:, p_, 512 * c + 128 * qb:512 * c + 128 * qb + 128]
                        qres = [r_q(p_, c)]
                        blocks = [(qb, qb, qb + 1) for qb in range(4)]
                    elif d == 4:
                        b_ = c
                        for s in range(5):
                            j0 = 128 * s - 64
                            kcol = 1024 + 4 * j0 + b_
                            edge = 'L' if s == 0 else ('R' if s == 4 else None)
                            tiles.append((slice(kcol, kcol + 4 * 127 + 1, 4), edge))
                        dmas.append((Vring[0:64, vg, 0, :], vrow_g(0, 2048 - 256 + b_, 4, 64, None, 0, p_)))
                        dmas.append((Vring[64:128, vg, 0, :], vrow_ap(pub_d, 0, b_, 4, 64, None, 0, p_)))
                        dmas.append((Vring[:, vg, 1:4, :], vrow_ap(pub_d, 0, 4 * 64 + b_, 4, 128, 3, 512, p_)))
                        dmas.append((Vring[0:64, vg, 4, :], vrow_ap(pub_d, 0, 4 * 448 + b_, 4, 64, None, 0, p_)))
                        dmas.append((Vring[64:128, vg, 4, :], vrow_g(1, b_, 4, 64, None, 0, p_)))
                        qsrc = lambda qb, c=c, kb=kb: Qp[:, kb, 0, 512 * c + 128 * qb:512 * c + 128 * qb + 128]
                        qres = qres4
                        blocks = [(qb, qb, qb + 1) for qb in range(4)]
                    else:
                        b_ = c
                        for a in range(4):
                            r_ = 4 * a + b_
                            kA = 1024 + 16 * (-64) + r_
                            kB = 1024 + 16 * 64 + r_
                            tiles.append((slice(kA, kA + 16 * 127 + 1, 16), 'L'))
                        for a in range(4):
                            r_ = 4 * a + b_
                            kB = 1024 + 16 * 64 + r_
                            tiles.append((slice(kB, kB + 16 * 127 + 1, 16), 'R'))
                        dmas.append((Vring[0:32, vg, 0:4, :], vrow_g(0, 1024 + b_, 16, 32, 4, 4, p_)))
                        dmas.append((Vring[32:64, vg, 0:4, :], vrow_g(0, 1536 + b_, 16, 32, 4, 4, p_)))
                        dmas.append((Vring[64:128, vg, 0:4, :], vrow_ap(pub_d, 0, b_, 16, 64, 4, 4, p_)))
                        dmas.append((Vring[0:64, vg, 4:8, :], vrow_ap(pub_d, 0, 16 * 64 + b_, 16, 64, 4, 4, p_)))
                        dmas.append((Vring[64:96, vg, 4:8, :], vrow_g(1, b_, 16, 32, 4, 4, p_)))
                        dmas.append((Vring[96:128, vg, 4:8, :], vrow_g(1, 512 + b_, 16, 32, 4, 4, p_)))
                        qsrc = lambda qb, c=c, kb=kb: Qp[:, kb, 1, 512 * c + 128 * qb:512 * c + 128 * qb + 128]
                        qres = qres16
                        blocks = [(a, a, 4 + a) for a in range(4)]
                    vall = [(vres, k_) for k_ in range(6)]

                    vgcnt[0] += 1

                    def pre(dmas=dmas, tiles=tiles, vg=vg, vres=vres, vall=vall, bid=vgcnt[0]):
                        if DBG_SKIP_V:
                            return
                        for k_, (dst, src) in enumerate(dmas):
                            S.add("sp", lambda e, dst=dst, src=src: e.dma_start(out=dst, in_=src),
                                  reads=gathV_all + pubV_all,
                                  writes=[(vres, k_)], dma=f"vg{vg}", batch=bid)

                        def f_vmask(e):
                            ins = None
                            for s_, (ks, edge) in enumerate(tiles):
                                if edge == 'L':
                                    ins = e.tensor_scalar(Vring[:, vg, s_, :], Vring[:, vg, s_, :], flags[:, 2:3], None, ALU.mult)
                                elif edge == 'R':
                                    ins = e.tensor_scalar(Vring[:, vg, s_, :], Vring[:, vg, s_, :], flags[:, 3:4], None, ALU.mult)
                            return ins
                    ab = acc_flip.next()
                    bO, bD = (6, 7)
                    for bi, (qb, s0, s1) in enumerate(blocks):
                        unit = dict(ci=ci, c=c, qb=qb, bi=bi, s=(s0, s1), tiles=tiles, vg=vg, vres=vall, qsrc=qsrc, qres=qres,
                                    kb=kb, kres=kres, bO=bO, bD=bD, p=p_, last=(bi == 3), first_cfg=(ci == 0), d=d, pre=(pre if bi == 0 else None))
                        pending.append(unit)

            def emit_qk(u):
                sbk = sbank_ring.next()
                u["sb"] = sbk

                def f(e, u=u, sbk=sbk):
                    ins = None
                    q = u["qsrc"](u["qb"])
                    for si, s in enumerate(u["s"]):
                        ks = u["tiles"][s][0]
                        for hh in range(2):
                            rows = slice(hh * 64, hh * 64 + 64)
                            ins = e.matmul(ps[2 * sbk + hh][:, si * 128:si * 128 + 128],
                                           Kp[rows, u["kb"], ks], q[rows, :], start=True, stop=True,
                                           tile_position=(hh * 64, 0))
                    return ins
                S.add("pe", f, reads=u["kres"] + u["qres"], writes=[r_ps(2 * sbk), r_ps(2 * sbk + 1)])

            def emit_rest(u):
                sbk = u["sb"]
                pi = u["pi"] = (emit_rest.cnt % 2)
                emit_rest.cnt += 1
                def f_exp(e, sbk=sbk, pi=pi):
                    e.activation(Pex[:, pi, 0:256], ps[2 * sbk][:, 0:256], AF.Exp, scale=0.125)
                    return e.activation(Pex[:, pi, 256:512], ps[2 * sbk + 1][:, 0:256], AF.Exp, scale=0.125)
                S.add("act", f_exp, reads=[r_ps(2 * sbk), r_ps(2 * sbk + 1)], writes=[("Pex", pi)])
                def f_wm(e, pi=pi, u=u):
                    edges = [u["tiles"][s_][1] for s_ in u["s"]]
                    if not any(edges):
                        return e.tensor_tensor(Pw[:, pi, :], Pex[:, pi, :], Wt[:, u["ci"], u["p"], :], ALU.mult)
                    ins = None
                    for k_ in range(2):
                        o_ = Pw[:, pi, :].rearrange("p (h s q) -> p h s q", h=2, s=2)[:, :, k_, :]
                        i_ = Pex[:, pi, :].rearrange("p (h s q) -> p h s q", h=2, s=2)[:, :, k_, :]
                        w_ = Wt[:, u["ci"], u["p"], :].rearrange("p (h s q) -> p h s q", h=2, s=2)[:, :, k_, :]
                        if edges[k_] == 'L':
                            ins = e.scalar_tensor_tensor(o_, i_, flags[:, 2:3], w_, ALU.mult, ALU.mult)
                        elif edges[k_] == 'R':
                            ins = e.scalar_tensor_tensor(o_, i_, flags[:, 3:4], w_, ALU.mult, ALU.mult)
                        else:
                            ins = e.tensor_tensor(o_, i_, w_, ALU.mult)
                    return ins
                S.add("dve", f_wm, reads=[("Pex", pi), "flags"] + [("Wt", u["ci"], u["p"], hh) for hh in range(2)], writes=[("Pw", pi)])

                def f_pv(e, u=u, pi=pi):
                    ins = None
                    cols = slice(u["bi"] * 128, u["bi"] * 128 + 128)
                    for si, s in enumerate(u["s"]):
                        edge = u["tiles"][s][1]
                        on = onesL if edge == 'L' else (onesR if edge == 'R' else ones_bf[:, 0:64])
                        for hh in range(2):
                            rows = slice(hh * 64, hh * 64 + 64)
                            pm = Pw[:, pi, hh * 256 + si * 128:hh * 256 + si * 128 + 128]
                            e.matmul(ps[u["bO"]][rows, cols], Vring[:, u["vg"], s, hh * 64:hh * 64 + 64], pm,
                                     start=(si == 0), stop=(si == 1), tile_position=(0, hh * 64))
                            ins = e.matmul(ps[u["bD"]][rows, cols], ones_bf[:, 0:64], pm,
                                           start=(si == 0), stop=(si == 1), tile_position=(0, hh * 64))
                    return ins
                if not DBG_SKIP_PV:
                    S.add("pe", f_pv, reads=[("Pw", pi), "ones", "onesL", "onesR"] + u["vres"], writes=[r_ps(u["bO"]), r_ps(u["bD"])])
                if u["last"] and not DBG_SKIP_PV:
                    ci, c = u["ci"], u["c"]
                    if ci == 0:
                        vo = accO[:, 512 * c:512 * (c + 1)]
                        vd = accD[:, 512 * c:512 * (c + 1)]
                        S.add("dve", lambda e, u=u, vo=vo: e.tensor_copy(vo, ps[u["bO"]][:, :]),
                              reads=[r_ps(u["bO"])], writes=[r_big(c)])
                        S.add("act", lambda e, u=u, vd=vd: e.activation(vd, ps[u["bD"]][:, :], AF.Copy),
                              reads=[r_ps(u["bD"])], writes=[r_big(4 + c)])
                    else:
                        if ci == 1:
                            vo = accO.rearrange("p (j b) -> p b j", b=4)[:, c, :]
                            vd = accD.rearrange("p (j b) -> p b j", b=4)[:, c, :]
                            pso = lambda bk: ps[bk][:, :]
                        else:
                            vo = accO.rearrange("p (i a b) -> p b a i", a=4, b=4)[:, c, :, :]
                            vd = accD.rearrange("p (i a b) -> p b a i", a=4, b=4)[:, c, :, :]
                            pso = lambda bk: ps[bk][:, :].rearrange("p (a i) -> p a i", a=4)
                        S.add("dve", lambda e, u=u, vo=vo, pso=pso: e.tensor_tensor(vo, pso(u["bO"]), vo, ALU.add),
                              reads=[r_ps(u["bO"])] + [r_big(k) for k in range(4)], writes=[r_big(k) for k in range(4)])
                        t1 = tmp_ring.next()
                        S.add("act", lambda e, u=u, t1=t1: e.activation(TMP[:, t1, :], ps[u["bD"]][:, :], AF.Copy),
                              reads=[r_ps(u["bD"])], writes=[r_tmp(t1)])
                        tsrc = TMP[:, t1, :] if ci == 1 else TMP[:, t1, :].rearrange("p (a i) -> p a i", a=4)
                        S.add("dve", lambda e, vd=vd, tsrc=tsrc: e.tensor_tensor(vd, tsrc, vd, ALU.add),
                              reads=[r_tmp(t1)] + [r_big(4 + k) for k in range(4)], writes=[r_big(4 + k) for k in range(4)])
            emit_rest.cnt = 0

            LOOK = 2
            if DBG_SKIP_UNITS:
                pending = []
            if pending and pending[0]["pre"]:
                pending[0]["pre"]()
            for i in range(len(pending) + LOOK):
                if i + 4 < len(pending) and pending[i + 4]["pre"]:
                    pending[i + 4]["pre"]()
                if i < len(pending):
                    emit_qk(pending[i])
                if i >= LOOK:
                    emit_rest(pending[i - LOOK])

            for c in range(4):
                t1 = tmp_ring.next()
                cs = slice(512 * c, 512 * (c + 1))

                S.add("act", lambda e, t1=t1, cs=cs: e.activation(TMP[:, t1, :], accD[:, cs], AF.Ln),
                      reads=[r_big(4 + c)], writes=[r_tmp(t1)])
                S.add("act", lambda e, t1=t1: e.activation(TMP[:, t1, :], TMP[:, t1, :], AF.Exp, scale=-1.0),
                      reads=[r_tmp(t1)], writes=[r_tmp(t1)])
                S.add("dve", lambda e, t1=t1, cs=cs: e.tensor_tensor(TMP[:, t1, :], accO[:, cs], TMP[:, t1, :], ALU.mult),
                      reads=[r_big(c), r_tmp(t1)], writes=[r_tmp(t1)])
                S.add("dve", lambda e, t1=t1, cs=cs, p_=p_: e.tensor_tensor(mix[:, p_, cs], TMP[:, t1, :], mix[:, p_, cs], ALU.mult),
                      reads=[r_tmp(t1), r_mix(p_, c)], writes=[r_mix(p_, c), r_big(c), r_big(4 + c)])

        checkpoint("attn")
        fix_reads = ["cnb", "cw", "flags"] + [("cub", f) for f in range(4)] + [("Bg", f, tt) for f in range(4) for tt in (0, 3)]

        def f_fix1(e, l=l):
            w0 = cwT[:, l, 0, :]; w1 = cwT[:, l, 1, :]; w2 = cwT[:, l, 2, :]
            e.tensor_scalar(fx[:, 0, :], cnb[:, 0, 3::4], flags[:, 0:1], None, ALU.mult)
            e.tensor_scalar(fx[:, 1, :], cnb[:, 1, 0::4], flags[:, 1:2], None, ALU.mult)
            e.tensor_tensor(fx[:, 2, :], cub[:, :, 0], w1, ALU.mult)
            e.tensor_tensor(fx[:, 3, :], cub[:, :, 1], w2, ALU.mult)
            e.tensor_tensor(fx[:, 4, :], cub[:, :, 3], w1, ALU.mult)
            return e.tensor_tensor(fx[:, 5, :], cub[:, :, 2], w0, ALU.mult)
        S.add("dve", f_fix1, reads=fix_reads, writes=["fx1"], sync_same=True)

        def f_fix2(e, l=l):
            w0 = cwT[:, l, 0, :]; w2 = cwT[:, l, 2, :]
            e.tensor_tensor(fx[:, 6, :], fx[:, 0, :], w0, ALU.mult)
            e.tensor_tensor(fx[:, 7, :], fx[:, 1, :], w2, ALU.mult)
            e.tensor_tensor(fx[:, 8, :], fx[:, 2, :], fx[:, 3, :], ALU.add)
            return e.tensor_tensor(fx[:, 9, :], fx[:, 4, :], fx[:, 5, :], ALU.add)
        S.add("dve", f_fix2, reads=["fx1", "cw"], writes=["fx2"], sync_same=True)

        def f_fix3(e):
            e.tensor_tensor(fx[:, 0, :], fx[:, 6, :], fx[:, 8, :], ALU.add)
            return e.tensor_tensor(fx[:, 1, :], fx[:, 7, :], fx[:, 9, :], ALU.add)
        S.add("dve", f_fix3, reads=["fx2", "fx1"], writes=["fx3"], sync_same=True)

        def f_fix4(e):
            e.tensor_tensor(fx[:, 2, :], fx[:, 0, :], Bg[:, :, 0], ALU.mult)
            return e.tensor_tensor(fx[:, 3, :], fx[:, 1, :], Bg[:, :, 1], ALU.mult)
        S.add("dve", f_fix4, reads=["fx3", "fx2", "fx1"] + fix_reads, writes=["fx4"], sync_same=True)

        def f_fix5(e):
            e.tensor_copy(mix[:, 4:8, 0], fx[:, 2, :])
            return e.tensor_copy(mix[:, 4:8, T - 1], fx[:, 3, :])
        S.add("dve", f_fix5, reads=["fx4"] + [r_mix(4 + f, tt) for f in range(4) for tt in (0, 3)],
              writes=[r_mix(4 + f, tt) for f in range(4) for tt in (0, 3)] + ["fx1"], sync_same=True)
        if STOP == "mixout":
            for ft in range(NFT):
                for tt in range(NTT):
                    S.add("dve", lambda e, ft=ft, tt=tt: e.tensor_copy(xT[:, ft, tt * TT:(tt + 1) * TT], mix[:, ft, tt * TT:(tt + 1) * TT]),
                          reads=[r_mix(ft, tt)], writes=[r_x(ft, tt)])
            raise _StopBuild()
        checkpoint("fix")
        wo = R1[:, 0:8192].rearrange("p (k c) -> p k c", k=8)
        S.add("pool", lambda e, l=l: e.dma_start(out=wo, in_=wout_d.ap()[l].rearrange("(ko ki) c -> ki ko c", ki=P)),
              writes=r_r1chunk(0) + r_r1chunk(1) + r_r1chunk(2) + r_r1chunk(3), dma="wo")
        wo_res = r_r1chunk(0) + r_r1chunk(1) + r_r1chunk(2) + r_r1chunk(3)
        o_sb = BIG[:].rearrange("p (c t) -> p c t", c=8)
        nxt = ada_tasks(l + 1) if l + 1 < NL else []
        for tt in range(NTT):
            tsl = slice(tt * TT, (tt + 1) * TT)
            for ct in range(8):
                if nxt:
                    nxt.pop(0)()
                bank = bank_ring.next()

                def f_o(e, ct=ct, tsl=tsl, bank=bank):
                    ins = None
                    for ko in range(8):
                        ins = e.matmul(ps[bank][:, :], wo[:, ko, ct * P:(ct + 1) * P], mix[:, ko, tsl], start=(ko == 0), stop=(ko == 7))
                    return ins
                S.add("pe", f_o, reads=wo_res + [r_mix(ko, tt) for ko in range(8)], writes=[r_ps(bank)])
                if ev_flip.next() == 0:
                    S.add("act", lambda e, ct=ct, bank=bank: e.activation(o_sb[:, ct, :], ps[bank][:, :], AF.Copy),
                          reads=[r_ps(bank)], writes=[r_big(ct)])
                else:
                    S.add("dve", lambda e, ct=ct, bank=bank: e.tensor_copy(o_sb[:, ct, :], ps[bank][:, :]),
                          reads=[r_ps(bank)], writes=[r_big(ct)])
            ri = rms_stats(lambda ft: o_sb[:, ft, :], lambda ft: r_big(ft), tt, 6)
            for ct in range(8):
                ti = tmp_ring.next()

                S.add("dve", lambda e, ct=ct, ti=ti, ri=ri: e.tensor_tensor(TMP[:, ti, :], o_sb[:, ct, :], rstdT[:, ri, :], ALU.mult),
                      reads=[r_big(ct), ("rstd", ri)], writes=[r_tmp(ti)])
                S.add("dve", lambda e, ct=ct, tsl=tsl, ti=ti, l=l: e.scalar_tensor_tensor(xT[:, ct, tsl], TMP[:, ti, :], ggate[:, l, ct:ct + 1], xT[:, ct, tsl], ALU.mult, ALU.add),
                      reads=[r_tmp(ti), ("ggate", l), r_x(ct, tt)], writes=[r_x(ct, tt)])
        while nxt:
            nxt.pop(0)()

    try:
        for l in range(NL):
            layer_body(l)
    except _StopBuild:
        pass
    for ft in range(NFT):
        i = S.add("sp", lambda e, ft=ft: e.dma_start(out=yT_d.ap()[ft * P:(ft + 1) * P, :], in_=xT[:, ft, :]),
                  reads=[r_x(ft, tt) for tt in range(NTT)], writes=[("y", ft)], dma=f"y{ft}")
        out_dmas.append(i)
    S.emit(final_waits=out_dmas)
    return nc


_PROG = {}


def _prep_inputs(x, c, w_ada, b_ada, pre_norm_g, w_in, conv_w, rel_bias, w_out, post_norm_g, l0, NL):
    onehot = make_onehot()
    sl = slice(l0, l0 + NL)
    wada = np.ascontiguousarray(w_ada[sl], np.float32)
    win = np.ascontiguousarray(w_in[sl], np.float32)
    wout = np.ascontiguousarray(w_out[sl], np.float32)
    badaT = np.ascontiguousarray(b_ada[sl].reshape(NL, 24, P).transpose(2, 0, 1), np.float32)
    gpreT = np.ascontiguousarray(pre_norm_g[sl].reshape(NL, 8, P).transpose(2, 0, 1), np.float32)
    gpostT = np.ascontiguousarray(post_norm_g[sl].reshape(NL, 8, P).transpose(2, 0, 1), np.float32)
    cwT = np.ascontiguousarray(conv_w[sl].reshape(NL, 3, 4, P).transpose(3, 0, 1, 2), np.float32)
    in_maps = []
    for core in range(NCORES):
        b, half = core // 2, core % 2
        xs = x[b, half * T:(half + 1) * T, :]
        fl = np.zeros((P, 4), np.float32)
        vprev = 1.0 if half == 1 else 0.0
        vnext = 1.0 if half == 0 else 0.0
        fl[:, 0] = vprev
        fl[:, 1] = vnext
        fl[:, 2] = 1.0
        fl[0:64, 2] = vprev
        fl[:, 3] = 1.0
        fl[64:128, 3] = vnext
        in_maps.append({
            "xT": np.ascontiguousarray(xs.T, np.float32),
            "cT": np.ascontiguousarray(c[b].reshape(8, P).T, np.float32),
            "w_ada": wada, "b_adaT": badaT, "gpreT": gpreT, "gpostT": gpostT,
            "w_in": win, "w_out": wout, "cwT": cwT,
            "rel_bias": np.ascontiguousarray(rel_bias, np.float32),
            "onehot": onehot, "flags": fl,
        })
    return in_maps


def _run(inputs, l0, NL, x_override=None):
    if NL not in _PROG:
        _PROG[NL] = build_program(NL)
    nc = _PROG[NL]
    x = inputs["x"] if x_override is None else x_override
    in_maps = _prep_inputs(x, inputs["c"], inputs["w_ada"], inputs["b_ada"], inputs["pre_norm_g"], inputs["w_in"],
                           inputs["conv_w"], inputs["rel_bias"], inputs["w_out"], inputs["post_norm_g"], l0, NL)
    res = run_bass_kernel_spmd(nc, in_maps, core_ids=list(range(NCORES)))
    B = x.shape[0]
    out = np.empty_like(np.asarray(x, np.float32))
    for core in range(NCORES):
        b, half = core // 2, core % 2
        out[b, half * T:(half + 1) * T, :] = res.results[core]["yT"].T
    return out


LAYERS_PER_LAUNCH = 4


def kernel(**inputs):
    inputs = {k: np.asarray(v) for k, v in inputs.items()}
    depth = inputs["w_in"].shape[0]
    x = np.asarray(inputs["x"], np.float32)
    for l0 in range(0, depth, LAYERS_PER_LAUNCH):
        x = _run(inputs, l0, LAYERS_PER_LAUNCH, x_override=x)
    return x.astype(np.float32)
```

```python
import math
import numpy as np
import concourse.bass as bass
import concourse.mybir as mybir
from concourse.bass_utils import run_bass_kernel_spmd

F32 = mybir.dt.float32
BF16 = mybir.dt.bfloat16
AF = mybir.ActivationFunctionType
ALU = mybir.AluOpType

P = 128
T = 2048
D = 1024
NFT = 8
TT = 512
NTT = 4
INC = 4096
EPS = 1e-6
NEG = -30000.0
NCORES = 8
DILS = (1, 4, 16)


class Sched:
    def __init__(self, nc):
        self.nc = nc
        self.ops = []
        self.lastw = {}
        self.readers = {}

    def add(self, eng, fn, reads=(), writes=(), dma=None, inc=16, sync_same=False, batch=None):
        idx = len(self.ops)
        deps = set()
        for r in reads:
            if r in self.lastw:
                deps.add(self.lastw[r])
        for w in writes:
            if w in self.lastw:
                deps.add(self.lastw[w])
            for rd in self.readers.get(w, ()):
                deps.add(rd)
        deps.discard(idx)
        for r in reads:
            self.readers.setdefault(r, []).append(idx)
        for w in writes:
            self.lastw[w] = idx
            self.readers[w] = []
        self.ops.append(dict(eng=eng, fn=fn, deps=deps, dma=dma, inc=inc, sync_same=sync_same, batch=batch))
        return idx

    def emit(self, final_waits=()):
        nc = self.nc
        ops = self.ops
        need = [False] * len(ops)
        for i, o in enumerate(ops):
            for d in o["deps"]:
                od = ops[d]
                if od["dma"] is not None:
                    continue
                if od["eng"] != o["eng"] or o["dma"] is not None or o["sync_same"] or o["eng"] != "pe":
                    need[d] = True
        for d in final_waits:
            if ops[d]["dma"] is None:
                need[d] = True
        engs = ["pe", "act", "dve", "pool", "sp"]
        esem = {e: nc.alloc_semaphore(name=f"sem_{e}") for e in engs}
        ssem = {}
        ecnt = {e: 0 for e in engs}
        scnt = {}
        token = [None] * len(ops)
        bfinal = {}
        for i, o in enumerate(ops):
            if o["dma"] is not None:
                s = o["dma"]
                if s not in ssem:
                    ssem[s] = nc.alloc_semaphore(name=f"sd_{len(ssem)}")
                    scnt[s] = 0
                scnt[s] += o["inc"]
                token[i] = (ssem[s], scnt[s], ("s", s))
                if o["batch"] is not None:
                    bfinal[(s, o["batch"])] = scnt[s]
            elif need[i]:
                ecnt[o["eng"]] += 1
                token[i] = (esem[o["eng"]], ecnt[o["eng"]], ("e", o["eng"]))
        for i, o in enumerate(ops):
            if o["dma"] is not None and o["batch"] is not None:
                sem, val, key = token[i]
                token[i] = (sem, bfinal[(o["dma"], o["batch"])], key)
        self.nsem = len(ssem) + len(engs)

        def emit_engine(e, handle):
            waited = {}
            for i, o in enumerate(ops):
                if o["eng"] != e:
                    continue
                wants = {}
                for d in sorted(o["deps"]):
                    od = ops[d]
                    if od["dma"] is None and od["eng"] == e and o["dma"] is None and not o["sync_same"] and e == "pe":
                        continue
                    sem, val, key = token[d]
                    if key not in wants or wants[key][1] < val:
                        wants[key] = (sem, val)
                for key, (sem, val) in wants.items():
                    if waited.get(key, 0) >= val:
                        continue
                    handle.wait_ge(sem, val)
                    waited[key] = val
                ins = o["fn"](handle)
                if token[i] is not None:
                    sem, val, key = token[i]
                    ins.then_inc(sem, o["inc"] if o["dma"] is not None else 1)
            if e == "sp":
                for d in final_waits:
                    sem, val, key = token[d]
                    if waited.get(key, 0) >= val:
                        continue
                    handle.wait_ge(sem, val)
                    waited[key] = val

        with nc.Block() as block:
            @block.tensor
            def _(h):
                emit_engine("pe", h)

            @block.scalar
            def _(h):
                emit_engine("act", h)

            @block.vector
            def _(h):
                emit_engine("dve", h)

            @block.gpsimd
            def _(h):
                emit_engine("pool", h)

            @block.sync
            def _(h):
                emit_engine("sp", h)


def t5_bucket_np(rel):
    rel = np.asarray(rel, np.int64)
    nb = 16
    me = 8
    ret = np.where(rel > 0, nb, 0)
    n = np.abs(rel)
    nf = np.maximum(n, 1).astype(np.float32)
    large = me + (np.log(nf / np.float32(me)) / np.float32(math.log(1024 / me))
                  * np.float32(nb - me)).astype(np.int32)
    large = np.minimum(large, nb - 1)
    return ret + np.where(n < me, n, large)


def make_onehot():
    oh = np.zeros((33, 3 * 384), np.float32)
    for ci, d in enumerate(DILS):
        for u in range(384):
            m = u - 191
            if abs(m) <= 64:
                oh[int(t5_bucket_np(m * d)), ci * 384 + u] = 1.0
            else:
                oh[32, ci * 384 + u] = 1.0
    return oh


class _StopBuild(Exception):
    pass


STOP = None
DBG_SKIP_V = False
DBG_SKIP_PV = False
DBG_SKIP_UNITS = False
DBG_SKIP_QPERM = False


def build_program(NL):
    nc = bass.Bass("TRN2", target_bir_lowering=False)
    S = Sched(nc)

    def checkpoint(name):
        if STOP == name:
            raise _StopBuild()

    def dram(name, shape, dt, kind="ExternalInput"):
        return nc.dram_tensor(name, list(shape), dt, kind=kind)

    xT_d = dram("xT", [D, T], F32)
    cT_d = dram("cT", [P, 8], F32)
    wada_d = dram("w_ada", [NL, D, 3 * D], F32)
    bada_d = dram("b_adaT", [P, NL, 24], F32)
    gpre_d = dram("gpreT", [P, NL, 8], F32)
    gpost_d = dram("gpostT", [P, NL, 8], F32)
    win_d = dram("w_in", [NL, D, INC], F32)
    wout_d = dram("w_out", [NL, D, D], F32)
    cw_d = dram("cwT", [P, NL, 3, 4], F32)
    relb_d = dram("rel_bias", [32, 8], F32)
    oh_d = dram("onehot", [33, 1152], F32)
    flags_d = dram("flags", [P, 6], F32)
    ident_d = dram("ident", [P, P], F32)
    mrow_d = dram("mrow", [1, 256], F32)
    yT_d = dram("yT", [D, T], F32, kind="ExternalOutput")
    pub_d = dram("pub", [1024, 2048], BF16, kind="Internal")
    gath_d = dram("gath", [2048, 2048], BF16, kind="Internal")
    pub2_d = dram("pub2", [P, 16], F32, kind="Internal")
    gath2_d = dram("gath2", [2 * P, 16], F32, kind="Internal")
    E_d = dram("Evec", [8, 1152], F32, kind="Internal")
    groups = [[0, 1], [2, 3], [4, 5], [6, 7]]

    sb = nc.alloc_sbuf_tensor
    xT = sb("xTs", [P, NFT, T], F32)
    R1 = sb("R1", [P, 16384], BF16)
    mix = sb("mix", [P, NFT, T], BF16)
    Qnat = sb("Qnat", [P, 4, T], BF16)
    Wt = sb("Wt", [P, 3, 4, 512], BF16)
    wring = sb("wring", [P, 2, 8, 256], BF16)
    Vring = sb("Vring", [P, 2, 8, 128], BF16)
    ident_bf = sb("ident_bf", [P, P], BF16)
    mrow_bf = sb("mrow_bf", [1, 256], BF16)
    mrow_f = sb("mrow_f", [1, 256], F32)
    Pw = sb("Pw", [P, 2, 1024], BF16)
    BIG = sb("BIG", [P, 4096], F32)
    TMP = sb("TMP", [P, 4, 512], F32)
    sq = sb("sq", [P, 2, 512], BF16)
    rstdT = sb("rstdT", [P, 1, 512], F32)
    stg = sb("stg", [P, 2, 512], BF16)
    ones_bf = sb("ones_bf", [P, P], BF16)
    epsT = sb("epsT", [P, 1], F32)
    flags = sb("flagsS", [P, 6], F32)
    cact = sb("cact", [P, 8], BF16)
    cTs = sb("cTs", [P, 8], F32)
    modT = sb("modT", [P, NL, 24], F32)
    badaT = sb("badaT", [P, NL, 24], F32)
    gpreT = sb("gpreTs", [P, NL, 8], F32)
    gpostT = sb("gpostTs", [P, NL, 8], F32)
    gmod = sb("gmod", [P, NL, 8], F32)
    ggate = sb("ggate", [P, NL, 8], F32)
    cwT = sb("cwTs", [P, NL, 3, 4], F32)
    cub = sb("cub", [P, 4, 4], F32)
    Bg = sb("Bg", [P, 4, 2], F32)
    cnb = sb("cnb", [P, 2, 16], F32)
    fx = sb("fx", [P, 10, 4], F32)
    relb = sb("relb", [33, 8], F32)

    ps = [nc.alloc_psum_tensor(f"ps{i}", [P, 512], F32) for i in range(8)]
    print("SBUF bytes remaining per partition:", nc.sbuf_bytes_remaining)

    hT = R1[:].rearrange("p (f t) -> p f t", f=8)

    def r_x(ft, tt): return ("x", ft, tt)
    def r_h(ft, tt): return ("R1", ft, tt)
    def r_r1chunk(c): return [("R1", c, q) for q in range(4)]
    def r_mix(ft, tt): return ("mix", ft, tt)
    def r_q(p, tt): return ("Qn", p, tt)
    def r_ps(i): return ("ps", i)
    def r_big(c): return ("BIG", c)
    def r_tmp(i): return ("TMP", i)

    class Ring:
        def __init__(self, n): self.n = n; self.i = 0
        def next(self):
            v = self.i % self.n; self.i += 1; return v
    tmp_ring = Ring(4)
    rstd_ring = Ring(1)
    sq_ring = Ring(2)
    w_ring = Ring(2)
    stg_ring = Ring(2)
    ev_flip = Ring(2)

    for ft in range(NFT):
        S.add("sp", lambda e, ft=ft: e.dma_start(out=xT[:, ft, :], in_=xT_d.ap()[ft * P:(ft + 1) * P, :]),
              writes=[r_x(ft, tt) for tt in range(NTT)], dma=f"x{ft}")
    small = [(cTs, cT_d, "cTs"), (badaT, bada_d, "bada"), (gpreT, gpre_d, "gpre"), (gpostT, gpost_d, "gpost"),
             (cwT, cw_d, "cw"), (flags, flags_d, "flags")]
    for t_, d_, nm in small:
        S.add("sp", lambda e, t_=t_, d_=d_: e.dma_start(out=t_[:], in_=d_.ap()), writes=[nm], dma="s_" + nm)
    S.add("sp", lambda e: e.dma_start(out=relb[0:32, :], in_=relb_d.ap()), writes=["relb"], dma="s_relb")
    S.add("sp", lambda e: e.dma_start(out=BIG[0:33, 0:1152], in_=oh_d.ap()), writes=[r_big(0), r_big(1), r_big(2)], dma="s_oh")
    S.add("dve", lambda e: e.memset(relb[32:33, :], NEG), writes=["relb32"])
    S.add("dve", lambda e: e.memset(ones_bf[:], 1.0), writes=["ones"])
    S.add("dve", lambda e: e.memset(epsT[:], EPS), writes=["eps"])
    S.add("act", lambda e: e.activation(cact[:], cTs[:], AF.Silu), reads=["cTs"], writes=["cact"])
    S.add("sp", lambda e: e.dma_start(out=TMP[:, 3, 0:P], in_=ident_d.ap()), writes=[r_tmp(3)], dma="s_ident")
    S.add("dve", lambda e: e.tensor_copy(ident_bf[:], TMP[:, 3, 0:P]), reads=[r_tmp(3)], writes=["ident"])
    S.add("sp", lambda e: e.dma_start(out=mrow_f[:], in_=mrow_d.ap()), writes=["mrow_f"], dma="s_mrow")
    S.add("dve", lambda e: e.tensor_copy(mrow_bf[:], mrow_f[:]), reads=["mrow_f"], writes=["mrow"])

    def f_E(e):
        ins = None
        for ci in range(3):
            ins = e.matmul(ps[6][0:8, ci * 128:ci * 128 + 128], relb[:, :], BIG[0:33, ci * 384:ci * 384 + 128], start=True, stop=True)
            ins = e.matmul(ps[7][0:8, ci * 128:ci * 128 + 128], relb[:, :], BIG[0:33, ci * 384 + 128:ci * 384 + 256], start=True, stop=True)
            ins = e.matmul(ps[5][0:8, ci * 128:ci * 128 + 128], relb[:, :], BIG[0:33, ci * 384 + 256:ci * 384 + 384], start=True, stop=True)
        return ins
    S.add("pe", f_E, reads=["relb", "relb32", r_big(0), r_big(1), r_big(2)], writes=[r_ps(5), r_ps(6), r_ps(7)])
    Esb = BIG[0:8, 2048:2048 + 1152].rearrange("p (c u) -> p c u", c=3)

    def f_Ecopy(e):
        ins = None
        for j, pi in enumerate((6, 7, 5)):
            ins = e.tensor_copy(Esb[:, :, j * 128:(j + 1) * 128], ps[pi][0:8, 0:384].rearrange("p (c u) -> p c u", c=3))
        return ins
    S.add("dve", f_Ecopy, reads=[r_ps(5), r_ps(6), r_ps(7)], writes=[r_big(4), r_big(5), r_big(6)])
    S.add("sp", lambda e: e.dma_start(out=E_d.ap(), in_=BIG[0:8, 2048:2048 + 1152]),
          reads=[r_big(4), r_big(5), r_big(6)], writes=["E_d"], dma="s_E")
    for ci in range(3):
        for h in range(8):
            ti = tmp_ring.next()
            p_, hh = h // 2, h % 2
            hank = bass.AP(E_d, h * 1152 + ci * 384, [[1, 128], [1, 256]])
            tv = TMP[:, ti, 0:256]
            S.add("sp", lambda e, tv=tv, hank=hank: e.dma_start(out=tv, in_=hank),
                  reads=["E_d"], writes=[r_tmp(ti)], dma=f"tmp{ti}")

            def f_W(e, ci=ci, p_=p_, hh=hh, ti=ti):
                e.activation(Wt[:, ci, p_, hh * 256:hh * 256 + 128], TMP[:, ti, 127::-1], AF.Identity)
                return e.activation(Wt[:, ci, p_, hh * 256 + 128:hh * 256 + 256], TMP[:, ti, 255:127:-1], AF.Identity)
            S.add("act", f_W, reads=[r_tmp(ti)], writes=[("Wt", ci, p_, hh)])
            if ci == 2:
                def f_Wm(e, p_=p_, hh=hh):
                    e.tensor_scalar(Wt[:, 2, p_, hh * 256:hh * 256 + 128], Wt[:, 2, p_, hh * 256:hh * 256 + 128], flags[:, 4:5], None, ALU.add)
                    return e.tensor_scalar(Wt[:, 2, p_, hh * 256 + 128:hh * 256 + 256], Wt[:, 2, p_, hh * 256 + 128:hh * 256 + 256], flags[:, 5:6], None, ALU.add)
                S.add("dve", f_Wm, reads=[("Wt", 2, p_, hh), "flags"], writes=[("Wt", 2, p_, hh)])

    def load_w(src_ap_kmajor, ncols):
        sl = w_ring.next()
        S.add("pool", lambda e: e.dma_start(out=wring[:, sl, :, 0:ncols],
                                            in_=src_ap_kmajor.rearrange("(ko ki) c -> ki ko c", ki=P)),
              writes=[("wr", sl)], dma=f"wr{sl}")
        return sl, ("wr", sl)

    def ada_tasks(l):
        tasks = []
        slots = {}

        def t_load(blk):
            def f():
                slots[blk] = load_w(wada_d.ap()[l, :, blk * 256:(blk + 1) * 256], 256)
            return f

        def t_mm(blk):
            def f():
                sl, rw = slots[blk]

                def f_mod(e):
                    ins = None
                    for j in range(2):
                        col = l * 24 + blk * 2 + j
                        for ko in range(8):
                            ins = e.matmul(ps[4][:, col:col + 1], wring[:, sl, ko, j * 128:(j + 1) * 128], cact[:, ko:ko + 1],
                                           start=(ko == 0), stop=(ko == 7))
                    return ins
                S.add("pe", f_mod, reads=[rw, "cact"], writes=[r_ps(4)])
            return f

        def t_fin():
            S.add("dve", lambda e: e.tensor_tensor(modT[:, l, :], ps[4][:, l * 24:(l + 1) * 24], badaT[:, l, :], ALU.add),
                  reads=[r_ps(4), "bada"], writes=[("modT", l)])
            S.add("dve", lambda e: e.scalar_tensor_tensor(gmod[:, l, :], modT[:, l, 8:16], 1.0, gpreT[:, l, :], ALU.add, ALU.mult),
                  reads=[("modT", l), "gpre"], writes=[("gmod", l)], sync_same=True)
            S.add("dve", lambda e: e.tensor_tensor(ggate[:, l, :], modT[:, l, 16:24], gpostT[:, l, :], ALU.mult),
                  reads=[("modT", l), "gpost"], writes=[("ggate", l)], sync_same=True)
        for blk in range(12):
            tasks.append(t_load(blk))
            if blk >= 1:
                tasks.append(t_mm(blk - 1))
        tasks.append(t_mm(11))
        tasks.append(t_fin)
        return tasks

    for t_ in ada_tasks(0):
        t_()

    def rms_stats(src_fn, src_res_fn, tt, ss_bank):
        for ft in range(NFT):
            si = sq_ring.next()
            S.add("act", lambda e, ft=ft, si=si: e.activation(sq[:, si, :], src_fn(ft), AF.Square),
                  reads=[src_res_fn(ft)], writes=[("sq", si)])
            S.add("pe", lambda e, ft=ft, si=si: e.matmul(ps[ss_bank][:, :], ones_bf[:, :], sq[:, si, :],
                                                         start=(ft == 0), stop=(ft == 7)),
                  reads=[("sq", si), "ones"], writes=[r_ps(ss_bank)])
        ti = rstd_ring.next()

        S.add("act", lambda e, ti=ti: e.activation(rstdT[:, ti, :], ps[ss_bank][:, :], AF.Ln, bias=epsT[:, 0:1], scale=1.0 / D),
              reads=[r_ps(ss_bank), "eps"], writes=[("rstd", ti)])
        S.add("act", lambda e, ti=ti: e.activation(rstdT[:, ti, :], rstdT[:, ti, :], AF.Exp, scale=-0.5),
              reads=[("rstd", ti)], writes=[("rstd", ti)])
        return ti

    bank_ring = Ring(4)

    out_dmas = []

    def layer_body(l):
        if STOP == "Eout":
            S.add("sp", lambda e: e.dma_start(out=BIG[0:8, 0:1152], in_=E_d.ap()), reads=["E_d"], writes=[r_big(0), r_big(1), r_big(2)], dma="dbgE")
            S.add("dve", lambda e: e.tensor_copy(xT[0:8, 0, 0:1152], BIG[0:8, 0:1152]), reads=[r_big(0), r_big(1), r_big(2)], writes=[r_x(0, 0), r_x(0, 1), r_x(0, 2)])
            for p2 in range(4):
                for ci2 in range(3):
                    S.add("dve", lambda e, p2=p2, ci2=ci2: e.tensor_copy(xT[:, 1 + p2, ci2 * 512:(ci2 + 1) * 512], Wt[:, ci2, p2, :]),
                          reads=[("Wt", ci2, p2, 0), ("Wt", ci2, p2, 1)], writes=[r_x(1 + p2, ci2)])
            raise _StopBuild()
        checkpoint("setup")
        for tt in range(NTT):
            tsl = slice(tt * TT, (tt + 1) * TT)
            ri = rms_stats(lambda ft, tsl=tsl: xT[:, ft, tsl], lambda ft, tt=tt: r_x(ft, tt), tt, 6)
            for ft in range(NFT):
                ti = tmp_ring.next()
                S.add("dve", lambda e, ft=ft, tsl=tsl, ti=ti, ri=ri: e.tensor_tensor(TMP[:, ti, :], xT[:, ft, tsl], rstdT[:, ri, :], ALU.mult),
                      reads=[r_x(ft, tt), ("rstd", ri)], writes=[r_tmp(ti)])
                S.add("act", lambda e, ft=ft, tsl=tsl, ti=ti, l=l: e.activation(hT[:, ft, tsl], TMP[:, ti, :], AF.Identity,
                                                                               bias=modT[:, l, ft:ft + 1], scale=gmod[:, l, ft:ft + 1]),
                      reads=[r_tmp(ti), ("gmod", l), ("modT", l)], writes=[r_h(ft, tt)])

        checkpoint("prenorm")
        def proj_tile(l, sl, rw, j, tt, bank):
            def f(e):
                ins = None
                for ko in range(8):
                    ins = e.matmul(ps[bank][:, :], wring[:, sl, ko, j * 128:(j + 1) * 128], hT[:, ko, tt * TT:(tt + 1) * TT],
                                   start=(ko == 0), stop=(ko == 7))
                return ins
            S.add("pe", f, reads=[rw] + [r_h(ko, tt) for ko in range(8)], writes=[r_ps(bank)])

        win = win_d.ap()
        for blk in range(2):
            sl, rw = load_w(win[l, :, 512 + blk * 256:512 + (blk + 1) * 256], 256)
            for j in range(2):
                p_ = blk * 2 + j
                for tt in range(NTT):
                    bank = bank_ring.next()
                    proj_tile(l, sl, rw, j, tt, bank)
                    si = stg_ring.next()
                    eng = "act" if ev_flip.next() == 0 else "dve"
                    if eng == "act":
                        S.add("act", lambda e, bank=bank, si=si: e.activation(stg[:, si, :], ps[bank][:, :], AF.Copy),
                              reads=[r_ps(bank)], writes=[("stg", si)])
                    else:
                        S.add("dve", lambda e, bank=bank, si=si: e.tensor_copy(stg[:, si, :], ps[bank][:, :]),
                              reads=[r_ps(bank)], writes=[("stg", si)])
                    S.add("sp", lambda e, si=si, p_=p_, tt=tt: e.dma_start(out=pub_d.ap()[p_ * P:(p_ + 1) * P, tt * TT:(tt + 1) * TT], in_=stg[:, si, :]),
                          reads=[("stg", si)], writes=[("pubK", p_, tt)], dma=f"stg{si}")
        for blk in range(2):
            sl, rw = load_w(win[l, :, 1024 + blk * 256:1024 + (blk + 1) * 256], 256)
            for tk in range(16):
                bank = bank_ring.next()

                def f_v(e, sl=sl, tk=tk, bank=bank):
                    ins = None
                    for ko in range(8):
                        ins = e.matmul(ps[bank][:, 0:256], hT[:, ko, tk * P:(tk + 1) * P], wring[:, sl, ko, :],
                                       start=(ko == 0), stop=(ko == 7))
                    return ins
                S.add("pe", f_v, reads=[rw] + [r_h(ko, tk // 4) for ko in range(8)], writes=[r_ps(bank)])
                si = stg_ring.next()
                eng = "act" if ev_flip.next() == 0 else "dve"
                if eng == "act":
                    S.add("act", lambda e, bank=bank, si=si: e.activation(stg[:, si, 0:256], ps[bank][:, 0:256], AF.Copy),
                          reads=[r_ps(bank)], writes=[("stg", si)])
                else:
                    S.add("dve", lambda e, bank=bank, si=si: e.tensor_copy(stg[:, si, 0:256], ps[bank][:, 0:256]),
                          reads=[r_ps(bank)], writes=[("stg", si)])
                vdst = bass.AP(pub_d, 512 * 2048 + tk * P * 512 + blk * 256, [[512, P], [1, 256]])
                S.add("sp", lambda e, si=si, vdst=vdst: e.dma_start(out=vdst, in_=stg[:, si, 0:256]),
                      reads=[("stg", si)], writes=[("pubV", tk, blk)], dma=f"stg{si}")
        checkpoint("kv")
        pending_cc = []
        for k in range(8):
            if k < 4:
                rr = [("pubK", k, tt) for tt in range(4)]
            else:
                rr = [("pubV", tk, b_) for tk in range(4 * (k - 4), 4 * (k - 4) + 4) for b_ in range(2)]

            def cc(k=k, rr=rr):
                S.add("pool", lambda e: e.collective_compute("AllGather", ALU.bypass, replica_groups=groups,
                                                             ins=[pub_d.ap()[k * P:(k + 1) * P, :]],
                                                             outs=[gath_d.ap()[k * 2 * P:(k + 1) * 2 * P, :]]),
                      reads=rr, writes=[("gath", k), "ccchain"], dma="cc1", inc=1)
            pending_cc.append(cc)

        def load_w2(src, ncols):
            r_ = load_w(src, ncols)
            if pending_cc:
                pending_cc.pop(0)()
            return r_

        checkpoint("cc1")
        for blk in range(2):
            sl, rw = load_w2(win[l, :, blk * 256:(blk + 1) * 256], 256)
            for j in range(2):
                p_ = blk * 2 + j
                for tt in range(NTT):
                    bank = bank_ring.next()
                    proj_tile(l, sl, rw, j, tt, bank)
                    if ev_flip.next() == 0:
                        S.add("act", lambda e, bank=bank, p_=p_, tt=tt: e.mul(Qnat[:, p_, tt * TT:(tt + 1) * TT], ps[bank][:, :], 0.125),
                              reads=[r_ps(bank)], writes=[r_q(p_, tt)])
                    else:
                        S.add("dve", lambda e, bank=bank, p_=p_, tt=tt: e.tensor_scalar(Qnat[:, p_, tt * TT:(tt + 1) * TT], ps[bank][:, :], 0.125, None, ALU.mult),
                              reads=[r_ps(bank)], writes=[r_q(p_, tt)])
        for blk in range(2):
            sl, rw = load_w2(win[l, :, 1536 + blk * 256:1536 + (blk + 1) * 256], 256)
            for j in range(2):
                p_ = blk * 2 + j
                for tt in range(NTT):
                    bank = bank_ring.next()
                    proj_tile(l, sl, rw, j, tt, bank)
                    S.add("act", lambda e, bank=bank, p_=p_, tt=tt: e.activation(mix[:, p_, tt * TT:(tt + 1) * TT], ps[bank][:, :], AF.Silu),
                          reads=[r_ps(bank)], writes=[r_mix(p_, tt)])
        checkpoint("qg")
        u_sb = BIG[:, 0:2048]
        cu = BIG[:, 2048:4096]
        for f in range(4):
            sl, rw = load_w2(win[l, :, 2048 + f * 128:2048 + (f + 1) * 128], 128)
            for tt in range(NTT):
                bank = bank_ring.next()
                proj_tile(l, sl, rw, 0, tt, bank)
                S.add("act", lambda e, bank=bank, tt=tt: e.activation(u_sb[:, tt * TT:(tt + 1) * TT], ps[bank][:, :], AF.Copy),
                      reads=[r_ps(bank)], writes=[r_big(tt)])
            sl, rw = load_w2(win[l, :, 3072 + f * 128:3072 + (f + 1) * 128], 128)
            for tt in range(NTT):
                bank = bank_ring.next()
                proj_tile(l, sl, rw, 0, tt, bank)
                S.add("dve", lambda e, bank=bank, tt=tt: e.tensor_tensor(cu[:, tt * TT:(tt + 1) * TT], ps[bank][:, :], u_sb[:, tt * TT:(tt + 1) * TT], ALU.mult),
                      reads=[r_ps(bank), r_big(tt)], writes=[r_big(4 + tt)])
            def f_cub(e, f=f):
                e.tensor_copy(cub[:, f, 0:2], cu[:, 0:2])
                return e.tensor_copy(cub[:, f, 2:4], cu[:, 2046:2048])
            S.add("dve", f_cub, reads=[r_big(4), r_big(7)], writes=[("cub", f)], sync_same=True)
            slB, rwB = load_w2(win[l, :, 2560 + f * 128:2560 + (f + 1) * 128], 128)
            bankB = []
            for tt in range(NTT):
                bank = bank_ring.next()
                proj_tile(l, slB, rwB, 0, tt, bank)
                t1 = tmp_ring.next()

                c0 = tt * TT
                acc = TMP[:, t1, :]
                cu_res = [r_big(4 + t_) for t_ in range(max(0, tt - 1), min(NTT, tt + 2))]
                S.add("dve", lambda e, acc=acc, c0=c0, l=l, f=f: e.tensor_scalar(acc, cu[:, c0:c0 + TT], cwT[:, l, 1, f:f + 1], None, ALU.mult),
                      reads=cu_res + ["cw"], writes=[r_tmp(t1)])
                if tt == 0:
                    S.add("dve", lambda e, acc=acc, l=l, f=f: e.scalar_tensor_tensor(acc[:, 1:TT], cu[:, 0:TT - 1], cwT[:, l, 0, f:f + 1], acc[:, 1:TT], ALU.mult, ALU.add),
                          reads=cu_res + ["cw", r_tmp(t1)], writes=[r_tmp(t1)])
                else:
                    S.add("dve", lambda e, acc=acc, c0=c0, l=l, f=f: e.scalar_tensor_tensor(acc, cu[:, c0 - 1:c0 + TT - 1], cwT[:, l, 0, f:f + 1], acc, ALU.mult, ALU.add),
                          reads=cu_res + ["cw", r_tmp(t1)], writes=[r_tmp(t1)])
                if tt == NTT - 1:
                    S.add("dve", lambda e, acc=acc, c0=c0, l=l, f=f: e.scalar_tensor_tensor(acc[:, 0:TT - 1], cu[:, c0 + 1:c0 + TT], cwT[:, l, 2, f:f + 1], acc[:, 0:TT - 1], ALU.mult, ALU.add),
                          reads=cu_res + ["cw", r_tmp(t1)], writes=[r_tmp(t1)])
                else:
                    S.add("dve", lambda e, acc=acc, c0=c0, l=l, f=f: e.scalar_tensor_tensor(acc, cu[:, c0 + 1:c0 + TT + 1], cwT[:, l, 2, f:f + 1], acc, ALU.mult, ALU.add),
                          reads=cu_res + ["cw", r_tmp(t1)], writes=[r_tmp(t1)])

                def f_convB(e, tt=tt, f=f, bank=bank, acc=acc):
                    if tt == 0:
                        e.tensor_copy(Bg[:, f, 0:1], ps[bank][:, 0:1])
                    if tt == NTT - 1:
                        e.tensor_copy(Bg[:, f, 1:2], ps[bank][:, TT - 1:TT])
                    return e.tensor_tensor(acc, ps[bank][:, :], acc, ALU.mult)
                S.add("dve", f_convB, reads=[r_ps(bank), r_tmp(t1)], writes=[r_tmp(t1), ("Bg", f, tt)])
                bankB.append(t1)
            slG, rwG = load_w2(win[l, :, 3584 + f * 128:3584 + (f + 1) * 128], 128)
            for tt in range(NTT):
                bank = bank_ring.next()
                proj_tile(l, slG, rwG, 0, tt, bank)
                t1 = bankB[tt]
                S.add("act", lambda e, bank=bank, tt=tt: e.activation(u_sb[:, tt * TT:(tt + 1) * TT], ps[bank][:, :], AF.Silu),
                      reads=[r_ps(bank), r_big(4 + tt)], writes=[r_big(tt)])

                def f_cm(e, tt=tt, f=f, t1=t1):
                    if tt == 0:
                        e.tensor_tensor(Bg[:, f, 0:1], Bg[:, f, 0:1], u_sb[:, 0:1], ALU.mult)
                    if tt == NTT - 1:
                        e.tensor_tensor(Bg[:, f, 1:2], Bg[:, f, 1:2], u_sb[:, T - 1:T], ALU.mult)
                    return e.tensor_tensor(mix[:, 4 + f, tt * TT:(tt + 1) * TT], TMP[:, t1, :], u_sb[:, tt * TT:(tt + 1) * TT], ALU.mult)
                S.add("dve", f_cm, reads=[r_tmp(t1), r_big(tt), ("Bg", f, tt)], writes=[r_mix(4 + f, tt), ("Bg", f, tt)])
        while pending_cc:
            pending_cc.pop(0)()
        checkpoint("conv")
        S.add("sp", lambda e: e.dma_start(out=pub2_d.ap(), in_=cub[:].rearrange("p f k -> p (f k)")),
              reads=[("cub", f) for f in range(4)], writes=["pub2"], dma="s_pub2")
        S.add("pool", lambda e: e.collective_compute("AllGather", ALU.bypass, replica_groups=groups,
                                                     ins=[pub2_d.ap()], outs=[gath2_d.ap()]),
              reads=["pub2"], writes=["gath2", "ccchain"], dma="cc2", inc=1)
        S.add("sp", lambda e: e.dma_start(out=cnb[:], in_=gath2_d.ap().rearrange("(r p) k -> p r k", r=2)),
              reads=["gath2"], writes=["cnb"], dma="s_cnb")
        conv_fix_layer = l

        checkpoint("cc2")
        Qp = R1[:, 0:8192].rearrange("p (b c t) -> p b c t", b=2, c=2)
        Kp = R1[:, 8192:16384].rearrange("p (b t) -> p b t", b=2)
        accO = BIG[:, 0:2048]
        accD = BIG[:, 2048:4096]
        pub = pub_d.ap()
        gath = gath_d.ap()
        Vpub_base = 512 * 2048

        def vrow_ap(tensor, rank_row0, tok0, tok_step, nrows, ntile, tile_step, p_):
            base = rank_row0 * 2048 + Vpub_base + tok0 * 512 + p_ * 128
            dims = [[tok_step * 512, nrows]]
            if ntile is not None:
                dims.append([tile_step * 512, ntile])
            dims.append([1, 128])
            return bass.AP(tensor, base, dims)

        def vrow_g(r, tok0, tok_step, nrows, ntile, tile_step, p_):
            last = tok0 + tok_step * (nrows - 1) + (tile_step * (ntile - 1) if ntile else 0)
            assert tok0 // 512 == last // 512, (tok0, last)
            base = ((4 + tok0 // 512) * 256 + r * 128) * 2048 + (tok0 % 512) * 512 + p_ * 128
            dims = [[tok_step * 512, nrows]]
            if ntile is not None:
                dims.append([tile_step * 512, ntile])
            dims.append([1, 128])
            return bass.AP(gath_d, base, dims)

        vg_ring = Ring(2)
        vgcnt = [l * 100000]
        pubV_all = [("pubV", tk, b2) for tk in range(16) for b2 in range(2)]
        gathV_all = [("gath", k) for k in range(4, 8)]
        sbank_ring = Ring(2)
        acc_flip = Ring(2)

        for p_ in range(4):
            kb = p_ % 2
            kres = r_r1chunk(4 + 2 * kb) + r_r1chunk(5 + 2 * kb)
            S.add("sp", lambda e, kb=kb, p_=p_: e.dma_start(out=Kp[:, kb, 0:1024], in_=gath[p_ * 2 * P:p_ * 2 * P + P, 1024:2048]),
                  reads=[("gath", p_)], writes=kres[0:2], dma=f"kp{kb}", batch=(l, p_))
            S.add("sp", lambda e, kb=kb, p_=p_: e.dma_start(out=Kp[:, kb, 1024:3072], in_=pub[p_ * P:(p_ + 1) * P, :]),
                  reads=[("pubK", p_, tt) for tt in range(4)], writes=kres[2:6], dma=f"kp{kb}", batch=(l, p_))
            S.add("sp", lambda e, kb=kb, p_=p_: e.dma_start(out=Kp[:, kb, 3072:4096], in_=gath[p_ * 2 * P + P:(p_ + 1) * 2 * P, 0:1024]),
                  reads=[("gath", p_)], writes=kres[6:8], dma=f"kp{kb}", batch=(l, p_))
            qres4 = r_r1chunk(2 * kb)
            qres16 = r_r1chunk(2 * kb + 1)
            if not DBG_SKIP_QPERM:
              S.add("dve", lambda e, kb=kb, p_=p_: e.tensor_copy(Qp[:, kb, 0, :].rearrange("p (b j) -> p b j", b=4),
                                                               Qnat[:, p_, :].rearrange("p (j b) -> p b j", b=4)),
                  reads=[r_q(p_, tt) for tt in range(4)], writes=qres4)
            if not DBG_SKIP_QPERM:
              S.add("act", lambda e, kb=kb, p_=p_: e.activation(Qp[:, kb, 1, :].rearrange("p (b a i) -> p b a i", b=4, a=4),
                                                              Qnat[:, p_, :].rearrange("p (i a b) -> p b a i", a=4, b=4), AF.Copy),
                  reads=[r_q(p_, tt) for tt in range(4)], writes=qres16)

            pending = []

            for ci, d in enumerate(DILS):
                for c in range(4):
                    vg = vg_ring.next()
                    vres = ("vg", vg)
                    tiles = []
                    dmas = []
                    if d == 1:
                        for s in range(5):
                            kt = 4 * c + s
                            kcol = 960 + 128 * kt
                            edge = 'L' if kt == 0 else ('R' if kt == 16 else None)
                            tiles.append((slice(kcol, kcol + 128, 1), edge))
                        lo = 0
                        if c == 0:
                            dmas.append((Vring[0:64, vg, 0, :], vrow_g(0, 1984, 1, 64, None, 0, p_)))
                            dmas.append((Vring[64:128, vg, 0, :], vrow_ap(pub_d, 0, 0, 1, 64, None, 0, p_)))
                            lo = 1
                        hi = 5
                        if c == 3:
                            dmas.append((Vring[0:64, vg, 4, :], vrow_ap(pub_d, 0, 1984, 1, 64, None, 0, p_)))
                            dmas.append((Vring[64:128, vg, 4, :], vrow_g(1, 0, 1, 64, None, 0, p_)))
                            hi = 4
                        kt0 = 4 * c + lo
                        dmas.append((Vring[:, vg, lo:hi, :], vrow_ap(pub_d, 0, 128 * kt0 - 64, 1, 128, hi - lo, 128, p_)))
                        qsrc = lambda qb, c=c, p_=p_: Qnat[:, p_, 512 * c + 128 * qb:512 * c + 128 * qb + 128]
                        qres = [r_q(p_, c)]
                        blocks = [(qb, qb, qb + 1) for qb in range(4)]
                    elif d == 4:
                        b_ = c
                        for s in range(5):
                            j0 = 128 * s - 64
                            kcol = 1024 + 4 * j0 + b_
                            edge = 'L' if s == 0 else ('R' if s == 4 else None)
                            tiles.append((slice(kcol, kcol + 4 * 127 + 1, 4), edge))
                        dmas.append((Vring[0:64, vg, 0, :], vrow_g(0, 2048 - 256 + b_, 4, 64, None, 0, p_)))
                        dmas.append((Vring[64:128, vg, 0, :], vrow_ap(pub_d, 0, b_, 4, 64, None, 0, p_)))
                        dmas.append((Vring[:, vg, 1:4, :], vrow_ap(pub_d, 0, 4 * 64 + b_, 4, 128, 3, 512, p_)))
                        dmas.append((Vring[0:64, vg, 4, :], vrow_ap(pub_d, 0, 4 * 448 + b_, 4, 64, None, 0, p_)))
                        dmas.append((Vring[64:128, vg, 4, :], vrow_g(1, b_, 4, 64, None, 0, p_)))
                        qsrc = lambda qb, c=c, kb=kb: Qp[:, kb, 0, 512 * c + 128 * qb:512 * c + 128 * qb + 128]
                        qres = qres4
                        blocks = [(qb, qb, qb + 1) for qb in range(4)]
                    else:
                        b_ = c
                        for a in range(4):
                            r_ = 4 * a + b_
                            kA = 1024 + 16 * (-64) + r_
                            kB = 1024 + 16 * 64 + r_
                            tiles.append((slice(kA, kA + 16 * 127 + 1, 16), 'L'))
                        for a in range(4):
                            r_ = 4 * a + b_
                            kB = 1024 + 16 * 64 + r_
                            tiles.append((slice(kB, kB + 16 * 127 + 1, 16), 'R'))
                        dmas.append((Vring[0:32, vg, 0:4, :], vrow_g(0, 1024 + b_, 16, 32, 4, 4, p_)))
                        dmas.append((Vring[32:64, vg, 0:4, :], vrow_g(0, 1536 + b_, 16, 32, 4, 4, p_)))
                        dmas.append((Vring[64:128, vg, 0:4, :], vrow_ap(pub_d, 0, b_, 16, 64, 4, 4, p_)))
                        dmas.append((Vring[0:64, vg, 4:8, :], vrow_ap(pub_d, 0, 16 * 64 + b_, 16, 64, 4, 4, p_)))
                        dmas.append((Vring[64:96, vg, 4:8, :], vrow_g(1, b_, 16, 32, 4, 4, p_)))
                        dmas.append((Vring[96:128, vg, 4:8, :], vrow_g(1, 512 + b_, 16, 32, 4, 4, p_)))
                        qsrc = lambda qb, c=c, kb=kb: Qp[:, kb, 1, 512 * c + 128 * qb:512 * c + 128 * qb + 128]
                        qres = qres16
                        blocks = [(a, a, 4 + a) for a in range(4)]
                    vall = [(vres, k_) for k_ in range(6)]

                    vgcnt[0] += 1

                    def pre(dmas=dmas, tiles=tiles, vg=vg, vres=vres, vall=vall, bid=vgcnt[0]):
                        if DBG_SKIP_V:
                            return
                        for k_, (dst, src) in enumerate(dmas):
                            S.add("sp", lambda e, dst=dst, src=src: e.dma_start(out=dst, in_=src),
                                  reads=gathV_all + pubV_all,
                                  writes=[(vres, k_)], dma=f"vg{vg}", batch=bid)

                        def f_vmask(e):
                            ins = None
                            for s_, (ks, edge) in enumerate(tiles):
                                if edge == 'L':
                                    ins = e.tensor_scalar(Vring[:, vg, s_, :], Vring[:, vg, s_, :], flags[:, 2:3], None, ALU.mult)
                                elif edge == 'R':
                                    ins = e.tensor_scalar(Vring[:, vg, s_, :], Vring[:, vg, s_, :], flags[:, 3:4], None, ALU.mult)
                            return ins
                    ab = acc_flip.next()
                    bO, bD = (4, 5) if ab == 0 else (6, 7)
                    for hf in range(2):
                        unit = dict(ci=ci, c=c, hf=hf, blks=blocks[2 * hf:2 * hf + 2], tiles=tiles, vg=vg, vres=vall, qsrc=qsrc, qres=qres,
                                    kb=kb, kres=kres, bO=bO, bD=bD, p=p_, last=(hf == 1), d=d, pre=(pre if hf == 0 else None))
                        pending.append(unit)

            def emit_qk(u):
                sbk = sbank_ring.next()
                u["sb"] = sbk

                def f(e, u=u, sbk=sbk):
                    ins = None
                    for bl, (qb, s0, s1) in enumerate(u["blks"]):
                        q = u["qsrc"](qb)
                        for hh in range(2):
                            ins = e.matmul(ps[2 * sbk + hh][:, bl * 256:bl * 256 + 256], ident_bf[:, :],
                                           Wt[:, u["ci"], u["p"], hh * 256:(hh + 1) * 256], start=True, stop=False)
                        for si, s_ in enumerate((s0, s1)):
                            ks = u["tiles"][s_][0]
                            for hh in range(2):
                                rows = slice(hh * 64, hh * 64 + 64)
                                c0_ = bl * 256 + si * 128
                                ins = e.matmul(ps[2 * sbk + hh][:, c0_:c0_ + 128],
                                               Kp[rows, u["kb"], ks], q[rows, :], start=False, stop=True,
                                               tile_position=(hh * 64, 0))
                        if u["ci"] != 2:
                            for si, s_ in enumerate((s0, s1)):
                                edge = u["tiles"][s_][1]
                                if edge:
                                    mo = 0 if edge == 'L' else 128
                                    for hh in range(2):
                                        c0_ = bl * 256 + si * 128
                                        ins = e.matmul(ps[2 * sbk + hh][:, c0_:c0_ + 128], mrow_bf[0:1, mo:mo + 128],
                                                       ones_bf[0:1, 0:128], start=False, stop=True)
                    return ins
                S.add("pe", f, reads=u["kres"] + u["qres"] + ["ident", "mrow", "ones"] + [("Wt", u["ci"], u["p"], hh) for hh in range(2)],
                      writes=[r_ps(2 * sbk), r_ps(2 * sbk + 1)])

            def emit_rest(u):
                sbk = u["sb"]
                pi = u["pi"] = (emit_rest.cnt % 2)
                emit_rest.cnt += 1
                def f_exp(e, sbk=sbk, pi=pi):
                    e.activation(Pw[:, pi, 0:512], ps[2 * sbk][:, :], AF.Exp)
                    return e.activation(Pw[:, pi, 512:1024], ps[2 * sbk + 1][:, :], AF.Exp)
                S.add("act", f_exp, reads=[r_ps(2 * sbk), r_ps(2 * sbk + 1)], writes=[("Pw", pi)])

                def f_pv(e, u=u, pi=pi):
                    ins = None
                    for bl, (qb, s0, s1) in enumerate(u["blks"]):
                        bi = 2 * u["hf"] + bl
                        cols = slice(bi * 128, bi * 128 + 128)
                        for si, s_ in enumerate((s0, s1)):
                            for hh in range(2):
                                rows = slice(hh * 64, hh * 64 + 64)
                                o_ = hh * 512 + bl * 256 + si * 128
                                pm = Pw[:, pi, o_:o_ + 128]
                                e.matmul(ps[u["bO"]][rows, cols], Vring[:, u["vg"], s_, hh * 64:hh * 64 + 64], pm,
                                         start=(si == 0), stop=(si == 1), tile_position=(0, hh * 64))
                                ins = e.matmul(ps[u["bD"]][rows, cols], ones_bf[:, 0:64], pm,
                                               start=(si == 0), stop=(si == 1), tile_position=(0, hh * 64))
                    return ins
                if not DBG_SKIP_PV:
                    S.add("pe", f_pv, reads=[("Pw", pi), "ones"] + u["vres"], writes=[r_ps(u["bO"]), r_ps(u["bD"])])
                if u["last"] and not DBG_SKIP_PV:
                    ci, c = u["ci"], u["c"]
                    if ci == 0:
                        vo = accO[:, 512 * c:512 * (c + 1)]
                        vd = accD[:, 512 * c:512 * (c + 1)]
                        S.add("dve", lambda e, u=u, vo=vo: e.tensor_copy(vo, ps[u["bO"]][:, :]),
                              reads=[r_ps(u["bO"])], writes=[r_big(c)])
                        S.add("act", lambda e, u=u, vd=vd: e.activation(vd, ps[u["bD"]][:, :], AF.Copy),
                              reads=[r_ps(u["bD"])], writes=[r_big(4 + c)])
                    else:
                        if ci == 1:
                            vo = accO.rearrange("p (j b) -> p b j", b=4)[:, c, :]
                            vd = accD.rearrange("p (j b) -> p b j", b=4)[:, c, :]
                            pso = lambda bk: ps[bk][:, :]
                        else:
                            vo = accO.rearrange("p (i a b) -> p b a i", a=4, b=4)[:, c, :, :]
                            vd = accD.rearrange("p (i a b) -> p b a i", a=4, b=4)[:, c, :, :]
                            pso = lambda bk: ps[bk][:, :].rearrange("p (a i) -> p a i", a=4)
                        S.add("dve", lambda e, u=u, vo=vo, pso=pso: e.tensor_tensor(vo, pso(u["bO"]), vo, ALU.add),
                              reads=[r_ps(u["bO"])] + [r_big(k) for k in range(4)], writes=[r_big(k) for k in range(4)])
                        t1 = tmp_ring.next()
                        S.add("act", lambda e, u=u, t1=t1: e.activation(TMP[:, t1, :], ps[u["bD"]][:, :], AF.Copy),
                              reads=[r_ps(u["bD"])], writes=[r_tmp(t1)])
                        tsrc = TMP[:, t1, :] if ci == 1 else TMP[:, t1, :].rearrange("p (a i) -> p a i", a=4)
                        S.add("dve", lambda e, vd=vd, tsrc=tsrc: e.tensor_tensor(vd, tsrc, vd, ALU.add),
                              reads=[r_tmp(t1)] + [r_big(4 + k) for k in range(4)], writes=[r_big(4 + k) for k in range(4)])
            emit_rest.cnt = 0

            LOOK = 1
            if DBG_SKIP_UNITS:
                pending = []
            if pending and pending[0]["pre"]:
                pending[0]["pre"]()
            for i in range(len(pending) + LOOK):
                if i < len(pending):
                    emit_qk(pending[i])
                if i >= LOOK:
                    emit_rest(pending[i - LOOK])
                if i + 2 < len(pending) and pending[i + 2]["pre"]:
                    pending[i + 2]["pre"]()

            for c in range(4):
                t1 = tmp_ring.next()
                cs = slice(512 * c, 512 * (c + 1))

                S.add("act", lambda e, t1=t1, cs=cs: e.activation(TMP[:, t1, :], accD[:, cs], AF.Ln),
                      reads=[r_big(4 + c)], writes=[r_tmp(t1)])
                S.add("act", lambda e, t1=t1: e.activation(TMP[:, t1, :], TMP[:, t1, :], AF.Exp, scale=-1.0),
                      reads=[r_tmp(t1)], writes=[r_tmp(t1)])
                S.add("dve", lambda e, t1=t1, cs=cs: e.tensor_tensor(TMP[:, t1, :], accO[:, cs], TMP[:, t1, :], ALU.mult),
                      reads=[r_big(c), r_tmp(t1)], writes=[r_tmp(t1)])
                S.add("dve", lambda e, t1=t1, cs=cs, p_=p_: e.tensor_tensor(mix[:, p_, cs], TMP[:, t1, :], mix[:, p_, cs], ALU.mult),
                      reads=[r_tmp(t1), r_mix(p_, c)], writes=[r_mix(p_, c), r_big(c), r_big(4 + c)])

        checkpoint("attn")
        fix_reads = ["cnb", "cw", "flags"] + [("cub", f) for f in range(4)] + [("Bg", f, tt) for f in range(4) for tt in (0, 3)]

        def f_fix1(e, l=l):
            w0 = cwT[:, l, 0, :]; w1 = cwT[:, l, 1, :]; w2 = cwT[:, l, 2, :]
            e.tensor_scalar(fx[:, 0, :], cnb[:, 0, 3::4], flags[:, 0:1], None, ALU.mult)
            e.tensor_scalar(fx[:, 1, :], cnb[:, 1, 0::4], flags[:, 1:2], None, ALU.mult)
            e.tensor_tensor(fx[:, 2, :], cub[:, :, 0], w1, ALU.mult)
            e.tensor_tensor(fx[:, 3, :], cub[:, :, 1], w2, ALU.mult)
            e.tensor_tensor(fx[:, 4, :], cub[:, :, 3], w1, ALU.mult)
            return e.tensor_tensor(fx[:, 5, :], cub[:, :, 2], w0, ALU.mult)
        S.add("dve", f_fix1, reads=fix_reads, writes=["fx1"], sync_same=True)

        def f_fix2(e, l=l):
            w0 = cwT[:, l, 0, :]; w2 = cwT[:, l, 2, :]
            e.tensor_tensor(fx[:, 6, :], fx[:, 0, :], w0, ALU.mult)
            e.tensor_tensor(fx[:, 7, :], fx[:, 1, :], w2, ALU.mult)
            e.tensor_tensor(fx[:, 8, :], fx[:, 2, :], fx[:, 3, :], ALU.add)
            return e.tensor_tensor(fx[:, 9, :], fx[:, 4, :], fx[:, 5, :], ALU.add)
        S.add("dve", f_fix2, reads=["fx1", "cw"], writes=["fx2"], sync_same=True)

        def f_fix3(e):
            e.tensor_tensor(fx[:, 0, :], fx[:, 6, :], fx[:, 8, :], ALU.add)
            return e.tensor_tensor(fx[:, 1, :], fx[:, 7, :], fx[:, 9, :], ALU.add)
        S.add("dve", f_fix3, reads=["fx2", "fx1"], writes=["fx3"], sync_same=True)

        def f_fix4(e):
            e.tensor_tensor(fx[:, 2, :], fx[:, 0, :], Bg[:, :, 0], ALU.mult)
            return e.tensor_tensor(fx[:, 3, :], fx[:, 1, :], Bg[:, :, 1], ALU.mult)
        S.add("dve", f_fix4, reads=["fx3", "fx2", "fx1"] + fix_reads, writes=["fx4"], sync_same=True)

        def f_fix5(e):
            e.tensor_copy(mix[:, 4:8, 0], fx[:, 2, :])
            return e.tensor_copy(mix[:, 4:8, T - 1], fx[:, 3, :])
        S.add("dve", f_fix5, reads=["fx4"] + [r_mix(4 + f, tt) for f in range(4) for tt in (0, 3)],
              writes=[r_mix(4 + f, tt) for f in range(4) for tt in (0, 3)] + ["fx1"], sync_same=True)
        if STOP == "mixout":
            for ft in range(NFT):
                for tt in range(NTT):
                    S.add("dve", lambda e, ft=ft, tt=tt: e.tensor_copy(xT[:, ft, tt * TT:(tt + 1) * TT], mix[:, ft, tt * TT:(tt + 1) * TT]),
                          reads=[r_mix(ft, tt)], writes=[r_x(ft, tt)])
            raise _StopBuild()
        checkpoint("fix")
        wo = R1[:, 0:8192].rearrange("p (k c) -> p k c", k=8)
        S.add("pool", lambda e, l=l: e.dma_start(out=wo, in_=wout_d.ap()[l].rearrange("(ko ki) c -> ki ko c", ki=P)),
              writes=r_r1chunk(0) + r_r1chunk(1) + r_r1chunk(2) + r_r1chunk(3), dma="wo")
        wo_res = r_r1chunk(0) + r_r1chunk(1) + r_r1chunk(2) + r_r1chunk(3)
        o_sb = BIG[:].rearrange("p (c t) -> p c t", c=8)
        nxt = ada_tasks(l + 1) if l + 1 < NL else []
        for tt in range(NTT):
            tsl = slice(tt * TT, (tt + 1) * TT)
            for ct in range(8):
                if nxt:
                    nxt.pop(0)()
                bank = bank_ring.next()

                def f_o(e, ct=ct, tsl=tsl, bank=bank):
                    ins = None
                    for ko in range(8):
                        ins = e.matmul(ps[bank][:, :], wo[:, ko, ct * P:(ct + 1) * P], mix[:, ko, tsl], start=(ko == 0), stop=(ko == 7))
                    return ins
                S.add("pe", f_o, reads=wo_res + [r_mix(ko, tt) for ko in range(8)], writes=[r_ps(bank)])
                if ev_flip.next() == 0:
                    S.add("act", lambda e, ct=ct, bank=bank: e.activation(o_sb[:, ct, :], ps[bank][:, :], AF.Copy),
                          reads=[r_ps(bank)], writes=[r_big(ct)])
                else:
                    S.add("dve", lambda e, ct=ct, bank=bank: e.tensor_copy(o_sb[:, ct, :], ps[bank][:, :]),
                          reads=[r_ps(bank)], writes=[r_big(ct)])
            ri = rms_stats(lambda ft: o_sb[:, ft, :], lambda ft: r_big(ft), tt, 6)
            for ct in range(8):
                ti = tmp_ring.next()

                S.add("dve", lambda e, ct=ct, ti=ti, ri=ri: e.tensor_tensor(TMP[:, ti, :], o_sb[:, ct, :], rstdT[:, ri, :], ALU.mult),
                      reads=[r_big(ct), ("rstd", ri)], writes=[r_tmp(ti)])
                S.add("dve", lambda e, ct=ct, tsl=tsl, ti=ti, l=l: e.scalar_tensor_tensor(xT[:, ct, tsl], TMP[:, ti, :], ggate[:, l, ct:ct + 1], xT[:, ct, tsl], ALU.mult, ALU.add),
                      reads=[r_tmp(ti), ("ggate", l), r_x(ct, tt)], writes=[r_x(ct, tt)])
        while nxt:
            nxt.pop(0)()

    try:
        for l in range(NL):
            layer_body(l)
    except _StopBuild:
        pass
    for ft in range(NFT):
        i = S.add("sp", lambda e, ft=ft: e.dma_start(out=yT_d.ap()[ft * P:(ft + 1) * P, :], in_=xT[:, ft, :]),
                  reads=[r_x(ft, tt) for tt in range(NTT)], writes=[("y", ft)], dma=f"y{ft}")
        out_dmas.append(i)
    S.emit(final_waits=out_dmas)
    return nc


_PROG = {}


def _prep_inputs(x, c, w_ada, b_ada, pre_norm_g, w_in, conv_w, rel_bias, w_out, post_norm_g, l0, NL):
    onehot = make_onehot()
    sl = slice(l0, l0 + NL)
    wada = np.ascontiguousarray(w_ada[sl], np.float32)
    win = np.ascontiguousarray(w_in[sl], np.float32)
    wout = np.ascontiguousarray(w_out[sl], np.float32)
    badaT = np.ascontiguousarray(b_ada[sl].reshape(NL, 24, P).transpose(2, 0, 1), np.float32)
    gpreT = np.ascontiguousarray(pre_norm_g[sl].reshape(NL, 8, P).transpose(2, 0, 1), np.float32)
    gpostT = np.ascontiguousarray(post_norm_g[sl].reshape(NL, 8, P).transpose(2, 0, 1), np.float32)
    cwT = np.ascontiguousarray(conv_w[sl].reshape(NL, 3, 4, P).transpose(3, 0, 1, 2), np.float32)
    in_maps = []
    for core in range(NCORES):
        b, half = core // 2, core % 2
        xs = x[b, half * T:(half + 1) * T, :]
        fl = np.zeros((P, 6), np.float32)
        vprev = 1.0 if half == 1 else 0.0
        vnext = 1.0 if half == 0 else 0.0
        fl[:, 0] = vprev
        fl[:, 1] = vnext
        fl[:, 2] = 1.0
        fl[0:64, 2] = vprev
        fl[:, 3] = 1.0
        fl[64:128, 3] = vnext
        fl[0:64, 4] = 0.0 if half == 1 else NEG
        fl[64:128, 5] = 0.0 if half == 0 else NEG
        in_maps.append({
            "xT": np.ascontiguousarray(xs.T, np.float32),
            "cT": np.ascontiguousarray(c[b].reshape(8, P).T, np.float32),
            "w_ada": wada, "b_adaT": badaT, "gpreT": gpreT, "gpostT": gpostT,
            "w_in": win, "w_out": wout, "cwT": cwT,
            "rel_bias": np.ascontiguousarray(rel_bias, np.float32),
            "onehot": onehot, "flags": fl,
            "ident": np.eye(P, dtype=np.float32),
            "mrow": np.concatenate([fl[:, 4], fl[:, 5]])[None, :].astype(np.float32),
        })
    return in_maps


def _run(inputs, l0, NL, x_override=None):
    if NL not in _PROG:
        _PROG[NL] = build_program(NL)
    nc = _PROG[NL]
    x = inputs["x"] if x_override is None else x_override
    in_maps = _prep_inputs(x, inputs["c"], inputs["w_ada"], inputs["b_ada"], inputs["pre_norm_g"], inputs["w_in"],
                           inputs["conv_w"], inputs["rel_bias"], inputs["w_out"], inputs["post_norm_g"], l0, NL)
    res = run_bass_kernel_spmd(nc, in_maps, core_ids=list(range(NCORES)))
    B = x.shape[0]
    out = np.empty_like(np.asarray(x, np.float32))
    for core in range(NCORES):
        b, half = core // 2, core % 2
        out[b, half * T:(half + 1) * T, :] = res.results[core]["yT"].T
    return out


LAYERS_PER_LAUNCH = 4


def kernel(**inputs):
    inputs = {k: np.asarray(v) for k, v in inputs.items()}
    depth = inputs["w_in"].shape[0]
    x = np.asarray(inputs["x"], np.float32)
    for l0 in range(0, depth, LAYERS_PER_LAUNCH):
        x = _run(inputs, l0, LAYERS_PER_LAUNCH, x_override=x)
    return x.astype(np.float32)
```

```python
import math
import numpy as np
import concourse.bass as bass
import concourse.mybir as mybir
from concourse.bass_utils import run_bass_kernel_spmd

F32 = mybir.dt.float32
BF16 = mybir.dt.bfloat16
AF = mybir.ActivationFunctionType
ALU = mybir.AluOpType

P = 128
T = 2048
D = 1024
NFT = 8
TT = 512
NTT = 4
INC = 4096
EPS = 1e-6
NEG = -30000.0
NCORES = 8
DILS = (1, 4, 16)


class Sched:
    def __init__(self, nc):
        self.nc = nc
        self.ops = []
        self.lastw = {}
        self.readers = {}

    def add(self, eng, fn, reads=(), writes=(), dma=None, inc=16, sync_same=False, batch=None):
        idx = len(self.ops)
        deps = set()
        for r in reads:
            if r in self.lastw:
                deps.add(self.lastw[r])
        for w in writes:
            if w in self.lastw:
                deps.add(self.lastw[w])
            for rd in self.readers.get(w, ()):
                deps.add(rd)
        deps.discard(idx)
        for r in reads:
            self.readers.setdefault(r, []).append(idx)
        for w in writes:
            self.lastw[w] = idx
            self.readers[w] = []
        self.ops.append(dict(eng=eng, fn=fn, deps=deps, dma=dma, inc=inc, sync_same=sync_same, batch=batch))
        return idx

    def emit(self, final_waits=()):
        nc = self.nc
        ops = self.ops
        need = [False] * len(ops)
        for i, o in enumerate(ops):
            for d in o["deps"]:
                od = ops[d]
                if od["dma"] is not None:
                    continue
                if od["eng"] != o["eng"] or o["dma"] is not None or o["sync_same"] or o["eng"] != "pe":
                    need[d] = True
        for d in final_waits:
            if ops[d]["dma"] is None:
                need[d] = True
        engs = ["pe", "act", "dve", "pool", "sp"]
        esem = {e: nc.alloc_semaphore(name=f"sem_{e}") for e in engs}
        ssem = {}
        ecnt = {e: 0 for e in engs}
        scnt = {}
        token = [None] * len(ops)
        bfinal = {}
        for i, o in enumerate(ops):
            if o["dma"] is not None:
                s = o["dma"]
                if s not in ssem:
                    ssem[s] = nc.alloc_semaphore(name=f"sd_{len(ssem)}")
                    scnt[s] = 0
                scnt[s] += o["inc"]
                token[i] = (ssem[s], scnt[s], ("s", s))
                if o["batch"] is not None:
                    bfinal[(s, o["batch"])] = scnt[s]
            elif need[i]:
                ecnt[o["eng"]] += 1
                token[i] = (esem[o["eng"]], ecnt[o["eng"]], ("e", o["eng"]))
        for i, o in enumerate(ops):
            if o["dma"] is not None and o["batch"] is not None:
                sem, val, key = token[i]
                token[i] = (sem, bfinal[(o["dma"], o["batch"])], key)
        self.nsem = len(ssem) + len(engs)

        def emit_engine(e, handle):
            waited = {}
            for i, o in enumerate(ops):
                if o["eng"] != e:
                    continue
                wants = {}
                for d in sorted(o["deps"]):
                    od = ops[d]
                    if od["dma"] is None and od["eng"] == e and o["dma"] is None and not o["sync_same"] and e == "pe":
                        continue
                    sem, val, key = token[d]
                    if key not in wants or wants[key][1] < val:
                        wants[key] = (sem, val)
                for key, (sem, val) in wants.items():
                    if waited.get(key, 0) >= val:
                        continue
                    handle.wait_ge(sem, val)
                    waited[key] = val
                ins = o["fn"](handle)
                if token[i] is not None:
                    sem, val, key = token[i]
                    ins.then_inc(sem, o["inc"] if o["dma"] is not None else 1)
            if e == "sp":
                for d in final_waits:
                    sem, val, key = token[d]
                    if waited.get(key, 0) >= val:
                        continue
                    handle.wait_ge(sem, val)
                    waited[key] = val

        with nc.Block() as block:
            @block.tensor
            def _(h):
                emit_engine("pe", h)

            @block.scalar
            def _(h):
                emit_engine("act", h)

            @block.vector
            def _(h):
                emit_engine("dve", h)

            @block.gpsimd
            def _(h):
                emit_engine("pool", h)

            @block.sync
            def _(h):
                emit_engine("sp", h)


def t5_bucket_np(rel):
    rel = np.asarray(rel, np.int64)
    nb = 16
    me = 8
    ret = np.where(rel > 0, nb, 0)
    n = np.abs(rel)
    nf = np.maximum(n, 1).astype(np.float32)
    large = me + (np.log(nf / np.float32(me)) / np.float32(math.log(1024 / me))
                  * np.float32(nb - me)).astype(np.int32)
    large = np.minimum(large, nb - 1)
    return ret + np.where(n < me, n, large)


def make_onehot():
    oh = np.zeros((33, 3 * 384), np.float32)
    for ci, d in enumerate(DILS):
        for u in range(384):
            m = u - 191
            if abs(m) <= 64:
                oh[int(t5_bucket_np(m * d)), ci * 384 + u] = 1.0
            else:
                oh[32, ci * 384 + u] = 1.0
    return oh


class _StopBuild(Exception):
    pass


STOP = None
DBG_SKIP_V = False
DBG_SKIP_PV = False
DBG_SKIP_UNITS = False
DBG_SKIP_QPERM = False


def build_program(NL):
    nc = bass.Bass("TRN2", target_bir_lowering=False)
    S = Sched(nc)

    def checkpoint(name):
        if STOP == name:
            raise _StopBuild()

    def dram(name, shape, dt, kind="ExternalInput"):
        return nc.dram_tensor(name, list(shape), dt, kind=kind)

    xT_d = dram("xT", [D, T], F32)
    cT_d = dram("cT", [P, 8], F32)
    wada_d = dram("w_ada", [NL, D, 3 * D], F32)
    bada_d = dram("b_adaT", [P, NL, 24], F32)
    gpre_d = dram("gpreT", [P, NL, 8], F32)
    gpost_d = dram("gpostT", [P, NL, 8], F32)
    win_d = dram("w_in", [NL, D, INC], F32)
    wout_d = dram("w_out", [NL, D, D], F32)
    cw_d = dram("cwT", [P, NL, 3, 4], F32)
    relb_d = dram("rel_bias", [32, 8], F32)
    oh_d = dram("onehot", [33, 1152], F32)
    flags_d = dram("flags", [P, 6], F32)
    ident_d = dram("ident", [P, P], F32)
    mrow_d = dram("mrow", [1, 256], F32)
    yT_d = dram("yT", [D, T], F32, kind="ExternalOutput")
    pub_d = dram("pub", [1024, 2048], BF16, kind="Internal")
    gath_d = dram("gath", [2048, 2048], BF16, kind="Internal")
    pub2_d = dram("pub2", [P, 16], F32, kind="Internal")
    gath2_d = dram("gath2", [2 * P, 16], F32, kind="Internal")
    E_d = dram("Evec", [8, 1152], F32, kind="Internal")
    groups = [[0, 1], [2, 3], [4, 5], [6, 7]]

    sb = nc.alloc_sbuf_tensor
    xT = sb("xTs", [P, NFT, T], F32)
    R1 = sb("R1", [P, 16384], BF16)
    mix = sb("mix", [P, NFT, T], BF16)
    Qnat = sb("Qnat", [P, 4, T], BF16)
    Wt = sb("Wt", [P, 3, 4, 512], BF16)
    wring = sb("wring", [P, 2, 8, 256], BF16)
    Vring = sb("Vring", [P, 3, 8, 128], BF16)
    ident_bf = sb("ident_bf", [P, P], BF16)
    mrow_bf = sb("mrow_bf", [1, 256], BF16)
    Pw = sb("Pw", [P, 2, 1024], BF16)
    BIG = sb("BIG", [P, 4096], F32)
    TMP = sb("TMP", [P, 4, 512], F32)
    sq = sb("sq", [P, 2, 512], BF16)
    rstdT = sb("rstdT", [P, 1, 512], F32)
    stg = sb("stg", [P, 2, 512], BF16)
    ones_bf = sb("ones_bf", [P, P], BF16)
    epsT = sb("epsT", [P, 1], F32)
    flags = sb("flagsS", [P, 6], F32)
    cact = sb("cact", [P, 8], BF16)
    cTs = sb("cTs", [P, 8], F32)
    modT = sb("modT", [P, NL, 24], F32)
    badaT = sb("badaT", [P, NL, 24], F32)
    gpreT = sb("gpreTs", [P, NL, 8], F32)
    gpostT = sb("gpostTs", [P, NL, 8], F32)
    gmod = sb("gmod", [P, NL, 8], F32)
    ggate = sb("ggate", [P, NL, 8], F32)
    cwT = sb("cwTs", [P, NL, 3, 4], F32)
    cub = sb("cub", [P, 4, 4], F32)
    Bg = sb("Bg", [P, 4, 2], F32)
    cnb = sb("cnb", [P, 2, 16], F32)
    fx = sb("fx", [P, 10, 4], F32)
    relb = sb("relb", [33, 8], F32)

    ps = [nc.alloc_psum_tensor(f"ps{i}", [P, 512], F32) for i in range(8)]
    print("SBUF bytes remaining per partition:", nc.sbuf_bytes_remaining)

    hT = R1[:].rearrange("p (f t) -> p f t", f=8)

    def r_x(ft, tt): return ("x", ft, tt)
    def r_h(ft, tt): return ("R1", ft, tt)
    def r_r1chunk(c): return [("R1", c, q) for q in range(4)]
    def r_mix(ft, tt): return ("mix", ft, tt)
    def r_q(p, tt): return ("Qn", p, tt)
    def r_ps(i): return ("ps", i)
    def r_big(c): return ("BIG", c)
    def r_tmp(i): return ("TMP", i)

    class Ring:
        def __init__(self, n): self.n = n; self.i = 0
        def next(self):
            v = self.i % self.n; self.i += 1; return v
    tmp_ring = Ring(4)
    rstd_ring = Ring(1)
    sq_ring = Ring(2)
    w_ring = Ring(2)
    stg_ring = Ring(2)
    ev_flip = Ring(2)

    for ft in range(NFT):
        S.add("sp", lambda e, ft=ft: e.dma_start(out=xT[:, ft, :], in_=xT_d.ap()[ft * P:(ft + 1) * P, :]),
              writes=[r_x(ft, tt) for tt in range(NTT)], dma=f"x{ft}")
    small = [(cTs, cT_d, "cTs"), (badaT, bada_d, "bada"), (gpreT, gpre_d, "gpre"), (gpostT, gpost_d, "gpost"),
             (cwT, cw_d, "cw"), (flags, flags_d, "flags")]
    for t_, d_, nm in small:
        S.add("sp", lambda e, t_=t_, d_=d_: e.dma_start(out=t_[:], in_=d_.ap()), writes=[nm], dma="s_" + nm)
    S.add("sp", lambda e: e.dma_start(out=relb[0:32, :], in_=relb_d.ap()), writes=["relb"], dma="s_relb")
    S.add("sp", lambda e: e.dma_start(out=BIG[0:33, 0:1152], in_=oh_d.ap()), writes=[r_big(0), r_big(1), r_big(2)], dma="s_oh")
    S.add("dve", lambda e: e.memset(relb[32:33, :], NEG), writes=["relb32"])
    S.add("dve", lambda e: e.memset(ones_bf[:], 1.0), writes=["ones"])
    S.add("dve", lambda e: e.memset(epsT[:], EPS), writes=["eps"])
    S.add("act", lambda e: e.activation(cact[:], cTs[:], AF.Silu), reads=["cTs"], writes=["cact"])
    S.add("sp", lambda e: e.dma_start(out=TMP[:, 3, 0:P], in_=ident_d.ap()), writes=[r_tmp(3)], dma="s_ident")
    S.add("dve", lambda e: e.tensor_copy(ident_bf[:], TMP[:, 3, 0:P]), reads=[r_tmp(3)], writes=["ident"])
    S.add("pool", lambda e: e.dma_start(out=mrow_bf[:], in_=mrow_d.ap()), writes=["mrow"], dma="s_mrow")

    def f_E(e):
        ins = None
        for ci in range(3):
            ins = e.matmul(ps[6][0:8, ci * 128:ci * 128 + 128], relb[:, :], BIG[0:33, ci * 384:ci * 384 + 128], start=True, stop=True)
            ins = e.matmul(ps[7][0:8, ci * 128:ci * 128 + 128], relb[:, :], BIG[0:33, ci * 384 + 128:ci * 384 + 256], start=True, stop=True)
            ins = e.matmul(ps[5][0:8, ci * 128:ci * 128 + 128], relb[:, :], BIG[0:33, ci * 384 + 256:ci * 384 + 384], start=True, stop=True)
        return ins
    S.add("pe", f_E, reads=["relb", "relb32", r_big(0), r_big(1), r_big(2)], writes=[r_ps(5), r_ps(6), r_ps(7)])
    Esb = BIG[0:8, 2048:2048 + 1152].rearrange("p (c u) -> p c u", c=3)

    def f_Ecopy(e):
        ins = None
        for j, pi in enumerate((6, 7, 5)):
            ins = e.tensor_copy(Esb[:, :, j * 128:(j + 1) * 128], ps[pi][0:8, 0:384].rearrange("p (c u) -> p c u", c=3))
        return ins
    S.add("dve", f_Ecopy, reads=[r_ps(5), r_ps(6), r_ps(7)], writes=[r_big(4), r_big(5), r_big(6)])
    S.add("sp", lambda e: e.dma_start(out=E_d.ap(), in_=BIG[0:8, 2048:2048 + 1152]),
          reads=[r_big(4), r_big(5), r_big(6)], writes=["E_d"], dma="s_E")
    for ci in range(3):
        for h in range(8):
            ti = tmp_ring.next()
            p_, hh = h // 2, h % 2
            hank = bass.AP(E_d, h * 1152 + ci * 384, [[1, 128], [1, 256]])
            tv = TMP[:, ti, 0:256]
            S.add("sp", lambda e, tv=tv, hank=hank: e.dma_start(out=tv, in_=hank),
                  reads=["E_d"], writes=[r_tmp(ti)], dma=f"tmp{ti}")

            def f_W(e, ci=ci, p_=p_, hh=hh, ti=ti):
                e.activation(Wt[:, ci, p_, hh * 256:hh * 256 + 128], TMP[:, ti, 127::-1], AF.Identity)
                return e.activation(Wt[:, ci, p_, hh * 256 + 128:hh * 256 + 256], TMP[:, ti, 255:127:-1], AF.Identity)
            S.add("act", f_W, reads=[r_tmp(ti)], writes=[("Wt", ci, p_, hh)])
            if ci == 2:
                def f_Wm(e, p_=p_, hh=hh):
                    e.tensor_scalar(Wt[:, 2, p_, hh * 256:hh * 256 + 128], Wt[:, 2, p_, hh * 256:hh * 256 + 128], flags[:, 4:5], None, ALU.add)
                    return e.tensor_scalar(Wt[:, 2, p_, hh * 256 + 128:hh * 256 + 256], Wt[:, 2, p_, hh * 256 + 128:hh * 256 + 256], flags[:, 5:6], None, ALU.add)
                S.add("dve", f_Wm, reads=[("Wt", 2, p_, hh), "flags"], writes=[("Wt", 2, p_, hh)])

    def load_w(src_ap_kmajor, ncols):
        sl = w_ring.next()
        S.add("pool", lambda e: e.dma_start(out=wring[:, sl, :, 0:ncols],
                                            in_=src_ap_kmajor.rearrange("(ko ki) c -> ki ko c", ki=P)),
              writes=[("wr", sl)], dma=f"wr{sl}")
        return sl, ("wr", sl)

    def ada_tasks(l):
        tasks = []
        slots = {}

        def t_load(blk):
            def f():
                slots[blk] = load_w(wada_d.ap()[l, :, blk * 256:(blk + 1) * 256], 256)
            return f

        def t_mm(blk):
            def f():
                sl, rw = slots[blk]

                def f_mod(e):
                    ins = None
                    for j in range(2):
                        col = l * 24 + blk * 2 + j
                        for ko in range(8):
                            ins = e.matmul(ps[4][:, col:col + 1], wring[:, sl, ko, j * 128:(j + 1) * 128], cact[:, ko:ko + 1],
                                           start=(ko == 0), stop=(ko == 7))
                    return ins
                S.add("pe", f_mod, reads=[rw, "cact"], writes=[r_ps(4)])
            return f

        def t_fin():
            S.add("dve", lambda e: e.tensor_tensor(modT[:, l, :], ps[4][:, l * 24:(l + 1) * 24], badaT[:, l, :], ALU.add),
                  reads=[r_ps(4), "bada"], writes=[("modT", l)])
            S.add("dve", lambda e: e.scalar_tensor_tensor(gmod[:, l, :], modT[:, l, 8:16], 1.0, gpreT[:, l, :], ALU.add, ALU.mult),
                  reads=[("modT", l), "gpre"], writes=[("gmod", l)], sync_same=True)
            S.add("dve", lambda e: e.tensor_tensor(ggate[:, l, :], modT[:, l, 16:24], gpostT[:, l, :], ALU.mult),
                  reads=[("modT", l), "gpost"], writes=[("ggate", l)], sync_same=True)
        for blk in range(12):
            tasks.append(t_load(blk))
            if blk >= 1:
                tasks.append(t_mm(blk - 1))
        tasks.append(t_mm(11))
        tasks.append(t_fin)
        return tasks

    for t_ in ada_tasks(0):
        t_()

    def rms_stats(src_fn, src_res_fn, tt, ss_bank):
        for ft in range(NFT):
            si = sq_ring.next()
            S.add("act", lambda e, ft=ft, si=si: e.activation(sq[:, si, :], src_fn(ft), AF.Square),
                  reads=[src_res_fn(ft)], writes=[("sq", si)])
            S.add("pe", lambda e, ft=ft, si=si: e.matmul(ps[ss_bank][:, :], ones_bf[:, :], sq[:, si, :],
                                                         start=(ft == 0), stop=(ft == 7)),
                  reads=[("sq", si), "ones"], writes=[r_ps(ss_bank)])
        ti = rstd_ring.next()

        S.add("act", lambda e, ti=ti: e.activation(rstdT[:, ti, :], ps[ss_bank][:, :], AF.Ln, bias=epsT[:, 0:1], scale=1.0 / D),
              reads=[r_ps(ss_bank), "eps"], writes=[("rstd", ti)])
        S.add("act", lambda e, ti=ti: e.activation(rstdT[:, ti, :], rstdT[:, ti, :], AF.Exp, scale=-0.5),
              reads=[("rstd", ti)], writes=[("rstd", ti)])
        return ti

    bank_ring = Ring(4)

    out_dmas = []

    def layer_body(l):
        if STOP == "Eout":
            S.add("sp", lambda e: e.dma_start(out=BIG[0:8, 0:1152], in_=E_d.ap()), reads=["E_d"], writes=[r_big(0), r_big(1), r_big(2)], dma="dbgE")
            S.add("dve", lambda e: e.tensor_copy(xT[0:8, 0, 0:1152], BIG[0:8, 0:1152]), reads=[r_big(0), r_big(1), r_big(2)], writes=[r_x(0, 0), r_x(0, 1), r_x(0, 2)])
            for p2 in range(4):
                for ci2 in range(3):
                    S.add("dve", lambda e, p2=p2, ci2=ci2: e.tensor_copy(xT[:, 1 + p2, ci2 * 512:(ci2 + 1) * 512], Wt[:, ci2, p2, :]),
                          reads=[("Wt", ci2, p2, 0), ("Wt", ci2, p2, 1)], writes=[r_x(1 + p2, ci2)])
            raise _StopBuild()
        checkpoint("setup")
        for tt in range(NTT):
            tsl = slice(tt * TT, (tt + 1) * TT)
            ri = rms_stats(lambda ft, tsl=tsl: xT[:, ft, tsl], lambda ft, tt=tt: r_x(ft, tt), tt, 6)
            for ft in range(NFT):
                ti = tmp_ring.next()
                S.add("dve", lambda e, ft=ft, tsl=tsl, ti=ti, ri=ri: e.tensor_tensor(TMP[:, ti, :], xT[:, ft, tsl], rstdT[:, ri, :], ALU.mult),
                      reads=[r_x(ft, tt), ("rstd", ri)], writes=[r_tmp(ti)])
                S.add("act", lambda e, ft=ft, tsl=tsl, ti=ti, l=l: e.activation(hT[:, ft, tsl], TMP[:, ti, :], AF.Identity,
                                                                               bias=modT[:, l, ft:ft + 1], scale=gmod[:, l, ft:ft + 1]),
                      reads=[r_tmp(ti), ("gmod", l), ("modT", l)], writes=[r_h(ft, tt)])

        checkpoint("prenorm")
        def proj_tile(l, sl, rw, j, tt, bank):
            def f(e):
                ins = None
                for ko in range(8):
                    ins = e.matmul(ps[bank][:, :], wring[:, sl, ko, j * 128:(j + 1) * 128], hT[:, ko, tt * TT:(tt + 1) * TT],
                                   start=(ko == 0), stop=(ko == 7))
                return ins
            S.add("pe", f, reads=[rw] + [r_h(ko, tt) for ko in range(8)], writes=[r_ps(bank)])

        win = win_d.ap()
        for blk in range(2):
            sl, rw = load_w(win[l, :, 512 + blk * 256:512 + (blk + 1) * 256], 256)
            for j in range(2):
                p_ = blk * 2 + j
                for tt in range(NTT):
                    bank = bank_ring.next()
                    proj_tile(l, sl, rw, j, tt, bank)
                    si = stg_ring.next()
                    eng = "act" if ev_flip.next() == 0 else "dve"
                    if eng == "act":
                        S.add("act", lambda e, bank=bank, si=si: e.activation(stg[:, si, :], ps[bank][:, :], AF.Copy),
                              reads=[r_ps(bank)], writes=[("stg", si)])
                    else:
                        S.add("dve", lambda e, bank=bank, si=si: e.tensor_copy(stg[:, si, :], ps[bank][:, :]),
                              reads=[r_ps(bank)], writes=[("stg", si)])
                    S.add("sp", lambda e, si=si, p_=p_, tt=tt: e.dma_start(out=pub_d.ap()[p_ * P:(p_ + 1) * P, tt * TT:(tt + 1) * TT], in_=stg[:, si, :]),
                          reads=[("stg", si)], writes=[("pubK", p_, tt)], dma=f"stg{si}")
        for blk in range(2):
            sl, rw = load_w(win[l, :, 1024 + blk * 256:1024 + (blk + 1) * 256], 256)
            for tk in range(16):
                bank = bank_ring.next()

                def f_v(e, sl=sl, tk=tk, bank=bank):
                    ins = None
                    for ko in range(8):
                        ins = e.matmul(ps[bank][:, 0:256], hT[:, ko, tk * P:(tk + 1) * P], wring[:, sl, ko, :],
                                       start=(ko == 0), stop=(ko == 7))
                    return ins
                S.add("pe", f_v, reads=[rw] + [r_h(ko, tk // 4) for ko in range(8)], writes=[r_ps(bank)])
                si = stg_ring.next()
                eng = "act" if ev_flip.next() == 0 else "dve"
                if eng == "act":
                    S.add("act", lambda e, bank=bank, si=si: e.activation(stg[:, si, 0:256], ps[bank][:, 0:256], AF.Copy),
                          reads=[r_ps(bank)], writes=[("stg", si)])
                else:
                    S.add("dve", lambda e, bank=bank, si=si: e.tensor_copy(stg[:, si, 0:256], ps[bank][:, 0:256]),
                          reads=[r_ps(bank)], writes=[("stg", si)])
                vdst = bass.AP(pub_d, 512 * 2048 + tk * P * 512 + blk * 256, [[512, P], [1, 256]])
                S.add("sp", lambda e, si=si, vdst=vdst: e.dma_start(out=vdst, in_=stg[:, si, 0:256]),
                      reads=[("stg", si)], writes=[("pubV", tk, blk)], dma=f"stg{si}")
        checkpoint("kv")
        pending_cc = []
        for k in range(8):
            if k < 4:
                rr = [("pubK", k, tt) for tt in range(4)]
            else:
                rr = [("pubV", tk, b_) for tk in range(4 * (k - 4), 4 * (k - 4) + 4) for b_ in range(2)]

            def cc(k=k, rr=rr):
                S.add("pool", lambda e: e.collective_compute("AllGather", ALU.bypass, replica_groups=groups,
                                                             ins=[pub_d.ap()[k * P:(k + 1) * P, :]],
                                                             outs=[gath_d.ap()[k * 2 * P:(k + 1) * 2 * P, :]]),
                      reads=rr, writes=[("gath", k), "ccchain"], dma="cc1", inc=1)
            pending_cc.append(cc)

        def load_w2(src, ncols):
            r_ = load_w(src, ncols)
            if pending_cc:
                pending_cc.pop(0)()
            return r_

        checkpoint("cc1")
        for blk in range(2):
            sl, rw = load_w2(win[l, :, blk * 256:(blk + 1) * 256], 256)
            for j in range(2):
                p_ = blk * 2 + j
                for tt in range(NTT):
                    bank = bank_ring.next()
                    proj_tile(l, sl, rw, j, tt, bank)
                    if ev_flip.next() == 0:
                        S.add("act", lambda e, bank=bank, p_=p_, tt=tt: e.mul(Qnat[:, p_, tt * TT:(tt + 1) * TT], ps[bank][:, :], 0.125),
                              reads=[r_ps(bank)], writes=[r_q(p_, tt)])
                    else:
                        S.add("dve", lambda e, bank=bank, p_=p_, tt=tt: e.tensor_scalar(Qnat[:, p_, tt * TT:(tt + 1) * TT], ps[bank][:, :], 0.125, None, ALU.mult),
                              reads=[r_ps(bank)], writes=[r_q(p_, tt)])
        for blk in range(2):
            sl, rw = load_w2(win[l, :, 1536 + blk * 256:1536 + (blk + 1) * 256], 256)
            for j in range(2):
                p_ = blk * 2 + j
                for tt in range(NTT):
                    bank = bank_ring.next()
                    proj_tile(l, sl, rw, j, tt, bank)
                    S.add("act", lambda e, bank=bank, p_=p_, tt=tt: e.activation(mix[:, p_, tt * TT:(tt + 1) * TT], ps[bank][:, :], AF.Silu),
                          reads=[r_ps(bank)], writes=[r_mix(p_, tt)])
        checkpoint("qg")
        u_sb = BIG[:, 0:2048]
        cu = BIG[:, 2048:4096]
        for f in range(4):
            sl, rw = load_w2(win[l, :, 2048 + f * 128:2048 + (f + 1) * 128], 128)
            for tt in range(NTT):
                bank = bank_ring.next()
                proj_tile(l, sl, rw, 0, tt, bank)
                S.add("act", lambda e, bank=bank, tt=tt: e.activation(u_sb[:, tt * TT:(tt + 1) * TT], ps[bank][:, :], AF.Copy),
                      reads=[r_ps(bank)], writes=[r_big(tt)])
            sl, rw = load_w2(win[l, :, 3072 + f * 128:3072 + (f + 1) * 128], 128)
            for tt in range(NTT):
                bank = bank_ring.next()
                proj_tile(l, sl, rw, 0, tt, bank)
                S.add("dve", lambda e, bank=bank, tt=tt: e.tensor_tensor(cu[:, tt * TT:(tt + 1) * TT], ps[bank][:, :], u_sb[:, tt * TT:(tt + 1) * TT], ALU.mult),
                      reads=[r_ps(bank), r_big(tt)], writes=[r_big(4 + tt)])
            def f_cub(e, f=f):
                e.tensor_copy(cub[:, f, 0:2], cu[:, 0:2])
                return e.tensor_copy(cub[:, f, 2:4], cu[:, 2046:2048])
            S.add("dve", f_cub, reads=[r_big(4), r_big(7)], writes=[("cub", f)], sync_same=True)
            slB, rwB = load_w2(win[l, :, 2560 + f * 128:2560 + (f + 1) * 128], 128)
            bankB = []
            for tt in range(NTT):
                bank = bank_ring.next()
                proj_tile(l, slB, rwB, 0, tt, bank)
                t1 = tmp_ring.next()

                c0 = tt * TT
                acc = TMP[:, t1, :]
                cu_res = [r_big(4 + t_) for t_ in range(max(0, tt - 1), min(NTT, tt + 2))]
                S.add("dve", lambda e, acc=acc, c0=c0, l=l, f=f: e.tensor_scalar(acc, cu[:, c0:c0 + TT], cwT[:, l, 1, f:f + 1], None, ALU.mult),
                      reads=cu_res + ["cw"], writes=[r_tmp(t1)])
                if tt == 0:
                    S.add("dve", lambda e, acc=acc, l=l, f=f: e.scalar_tensor_tensor(acc[:, 1:TT], cu[:, 0:TT - 1], cwT[:, l, 0, f:f + 1], acc[:, 1:TT], ALU.mult, ALU.add),
                          reads=cu_res + ["cw", r_tmp(t1)], writes=[r_tmp(t1)])
                else:
                    S.add("dve", lambda e, acc=acc, c0=c0, l=l, f=f: e.scalar_tensor_tensor(acc, cu[:, c0 - 1:c0 + TT - 1], cwT[:, l, 0, f:f + 1], acc, ALU.mult, ALU.add),
                          reads=cu_res + ["cw", r_tmp(t1)], writes=[r_tmp(t1)])
                if tt == NTT - 1:
                    S.add("dve", lambda e, acc=acc, c0=c0, l=l, f=f: e.scalar_tensor_tensor(acc[:, 0:TT - 1], cu[:, c0 + 1:c0 + TT], cwT[:, l, 2, f:f + 1], acc[:, 0:TT - 1], ALU.mult, ALU.add),
                          reads=cu_res + ["cw", r_tmp(t1)], writes=[r_tmp(t1)])
                else:
                    S.add("dve", lambda e, acc=acc, c0=c0, l=l, f=f: e.scalar_tensor_tensor(acc, cu[:, c0 + 1:c0 + TT + 1], cwT[:, l, 2, f:f + 1], acc, ALU.mult, ALU.add),
                          reads=cu_res + ["cw", r_tmp(t1)], writes=[r_tmp(t1)])

                def f_convB(e, tt=tt, f=f, bank=bank, acc=acc):
                    if tt == 0:
                        e.tensor_copy(Bg[:, f, 0:1], ps[bank][:, 0:1])
                    if tt == NTT - 1:
                        e.tensor_copy(Bg[:, f, 1:2], ps[bank][:, TT - 1:TT])
                    return e.tensor_tensor(acc, ps[bank][:, :], acc, ALU.mult)
                S.add("dve", f_convB, reads=[r_ps(bank), r_tmp(t1)], writes=[r_tmp(t1), ("Bg", f, tt)])
                bankB.append(t1)
            slG, rwG = load_w2(win[l, :, 3584 + f * 128:3584 + (f + 1) * 128], 128)
            for tt in range(NTT):
                bank = bank_ring.next()
                proj_tile(l, slG, rwG, 0, tt, bank)
                t1 = bankB[tt]
                S.add("act", lambda e, bank=bank, tt=tt: e.activation(u_sb[:, tt * TT:(tt + 1) * TT], ps[bank][:, :], AF.Silu),
                      reads=[r_ps(bank), r_big(4 + tt)], writes=[r_big(tt)])

                def f_cm(e, tt=tt, f=f, t1=t1):
                    if tt == 0:
                        e.tensor_tensor(Bg[:, f, 0:1], Bg[:, f, 0:1], u_sb[:, 0:1], ALU.mult)
                    if tt == NTT - 1:
                        e.tensor_tensor(Bg[:, f, 1:2], Bg[:, f, 1:2], u_sb[:, T - 1:T], ALU.mult)
                    return e.tensor_tensor(mix[:, 4 + f, tt * TT:(tt + 1) * TT], TMP[:, t1, :], u_sb[:, tt * TT:(tt + 1) * TT], ALU.mult)
                S.add("dve", f_cm, reads=[r_tmp(t1), r_big(tt), ("Bg", f, tt)], writes=[r_mix(4 + f, tt), ("Bg", f, tt)])
        while pending_cc:
            pending_cc.pop(0)()
        checkpoint("conv")
        S.add("sp", lambda e: e.dma_start(out=pub2_d.ap(), in_=cub[:].rearrange("p f k -> p (f k)")),
              reads=[("cub", f) for f in range(4)], writes=["pub2"], dma="s_pub2")
        S.add("pool", lambda e: e.collective_compute("AllGather", ALU.bypass, replica_groups=groups,
                                                     ins=[pub2_d.ap()], outs=[gath2_d.ap()]),
              reads=["pub2"], writes=["gath2", "ccchain"], dma="cc2", inc=1)
        S.add("sp", lambda e: e.dma_start(out=cnb[:], in_=gath2_d.ap().rearrange("(r p) k -> p r k", r=2)),
              reads=["gath2"], writes=["cnb"], dma="s_cnb")
        conv_fix_layer = l

        checkpoint("cc2")
        Qp = R1[:, 0:8192].rearrange("p (b c t) -> p b c t", b=2, c=2)
        Kp = R1[:, 8192:16384].rearrange("p (b t) -> p b t", b=2)
        accO = BIG[:, 0:2048]
        accD = BIG[:, 2048:4096]
        pub = pub_d.ap()
        gath = gath_d.ap()
        Vpub_base = 512 * 2048

        def vrow_ap(tensor, rank_row0, tok0, tok_step, nrows, ntile, tile_step, p_):
            base = rank_row0 * 2048 + Vpub_base + tok0 * 512 + p_ * 128
            dims = [[tok_step * 512, nrows]]
            if ntile is not None:
                dims.append([tile_step * 512, ntile])
            dims.append([1, 128])
            return bass.AP(tensor, base, dims)

        def vrow_g(r, tok0, tok_step, nrows, ntile, tile_step, p_):
            last = tok0 + tok_step * (nrows - 1) + (tile_step * (ntile - 1) if ntile else 0)
            assert tok0 // 512 == last // 512, (tok0, last)
            base = ((4 + tok0 // 512) * 256 + r * 128) * 2048 + (tok0 % 512) * 512 + p_ * 128
            dims = [[tok_step * 512, nrows]]
            if ntile is not None:
                dims.append([tile_step * 512, ntile])
            dims.append([1, 128])
            return bass.AP(gath_d, base, dims)

        vg_ring = Ring(3)
        vgcnt = [l * 100000]
        pubV_all = [("pubV", tk, b2) for tk in range(16) for b2 in range(2)]
        gathV_all = [("gath", k) for k in range(4, 8)]
        sbank_ring = Ring(2)
        acc_flip = Ring(2)

        def prep_pair(p_):
            kb = p_ % 2
            kres = r_r1chunk(4 + 2 * kb) + r_r1chunk(5 + 2 * kb)
            S.add("sp", lambda e, kb=kb, p_=p_: e.dma_start(out=Kp[:, kb, 0:1024], in_=gath[p_ * 2 * P:p_ * 2 * P + P, 1024:2048]),
                  reads=[("gath", p_)], writes=kres[0:2], dma=f"kp{kb}", batch=(l, p_))
            S.add("sp", lambda e, kb=kb, p_=p_: e.dma_start(out=Kp[:, kb, 1024:3072], in_=pub[p_ * P:(p_ + 1) * P, :]),
                  reads=[("pubK", p_, tt) for tt in range(4)], writes=kres[2:6], dma=f"kp{kb}", batch=(l, p_))
            S.add("sp", lambda e, kb=kb, p_=p_: e.dma_start(out=Kp[:, kb, 3072:4096], in_=gath[p_ * 2 * P + P:(p_ + 1) * 2 * P, 0:1024]),
                  reads=[("gath", p_)], writes=kres[6:8], dma=f"kp{kb}", batch=(l, p_))
            qres4 = r_r1chunk(2 * kb)
            qres16 = r_r1chunk(2 * kb + 1)
            if not DBG_SKIP_QPERM:
              S.add("pool", lambda e, kb=kb, p_=p_: e.tensor_copy(Qp[:, kb, 0, :].rearrange("p (b j) -> p b j", b=4),
                                                               Qnat[:, p_, :].rearrange("p (j b) -> p b j", b=4)),
                  reads=[r_q(p_, tt) for tt in range(4)], writes=qres4)
            if not DBG_SKIP_QPERM:
              S.add("pool", lambda e, kb=kb, p_=p_: e.tensor_copy(Qp[:, kb, 1, :].rearrange("p (b a i) -> p b a i", b=4, a=4),
                                                                Qnat[:, p_, :].rearrange("p (i a b) -> p b a i", a=4, b=4)),
                  reads=[r_q(p_, tt) for tt in range(4)], writes=qres16)


        prep_pair(0)
        for p_ in range(4):
            kb = p_ % 2
            kres = r_r1chunk(4 + 2 * kb) + r_r1chunk(5 + 2 * kb)
            qres4 = r_r1chunk(2 * kb)
            qres16 = r_r1chunk(2 * kb + 1)
            pending = []

            for ci, d in enumerate(DILS):
                for c in range(4):
                    vg = vg_ring.next()
                    vres = ("vg", vg)
                    tiles = []
                    dmas = []
                    if d == 1:
                        for s in range(5):
                            kt = 4 * c + s
                            kcol = 960 + 128 * kt
                            edge = 'L' if kt == 0 else ('R' if kt == 16 else None)
                            tiles.append((slice(kcol, kcol + 128, 1), edge))
                        lo = 0
                        if c == 0:
                            dmas.append((Vring[0:64, vg, 0, :], vrow_g(0, 1984, 1, 64, None, 0, p_)))
                            dmas.append((Vring[64:128, vg, 0, :], vrow_ap(pub_d, 0, 0, 1, 64, None, 0, p_)))
                            lo = 1
                        hi = 5
                        if c == 3:
                            dmas.append((Vring[0:64, vg, 4, :], vrow_ap(pub_d, 0, 1984, 1, 64, None, 0, p_)))
                            dmas.append((Vring[64:128, vg, 4, :], vrow_g(1, 0, 1, 64, None, 0, p_)))
                            hi = 4
                        kt0 = 4 * c + lo
                        dmas.append((Vring[:, vg, lo:hi, :], vrow_ap(pub_d, 0, 128 * kt0 - 64, 1, 128, hi - lo, 128, p_)))
                        qsrc = lambda qb, c=c, p_=p_: Qnat[:, p_, 512 * c + 128 * qb:512 * c + 128 * qb + 128]
                        qres = [r_q(p_, c)]
                        blocks = [(qb, qb, qb + 1) for qb in range(4)]
                    elif d == 4:
                        b_ = c
                        for s in range(5):
                            j0 = 128 * s - 64
                            kcol = 1024 + 4 * j0 + b_
                            edge = 'L' if s == 0 else ('R' if s == 4 else None)
                            tiles.append((slice(kcol, kcol + 4 * 127 + 1, 4), edge))
                        dmas.append((Vring[0:64, vg, 0, :], vrow_g(0, 2048 - 256 + b_, 4, 64, None, 0, p_)))
                        dmas.append((Vring[64:128, vg, 0, :], vrow_ap(pub_d, 0, b_, 4, 64, None, 0, p_)))
                        dmas.append((Vring[:, vg, 1:4, :], vrow_ap(pub_d, 0, 4 * 64 + b_, 4, 128, 3, 512, p_)))
                        dmas.append((Vring[0:64, vg, 4, :], vrow_ap(pub_d, 0, 4 * 448 + b_, 4, 64, None, 0, p_)))
                        dmas.append((Vring[64:128, vg, 4, :], vrow_g(1, b_, 4, 64, None, 0, p_)))
                        qsrc = lambda qb, c=c, kb=kb: Qp[:, kb, 0, 512 * c + 128 * qb:512 * c + 128 * qb + 128]
                        qres = qres4
                        blocks = [(qb, qb, qb + 1) for qb in range(4)]
                    else:
                        b_ = c
                        for a in range(4):
                            r_ = 4 * a + b_
                            kA = 1024 + 16 * (-64) + r_
                            kB = 1024 + 16 * 64 + r_
                            tiles.append((slice(kA, kA + 16 * 127 + 1, 16), 'L'))
                        for a in range(4):
                            r_ = 4 * a + b_
                            kB = 1024 + 16 * 64 + r_
                            tiles.append((slice(kB, kB + 16 * 127 + 1, 16), 'R'))
                        dmas.append((Vring[0:32, vg, 0:4, :], vrow_g(0, 1024 + b_, 16, 32, 4, 4, p_)))
                        dmas.append((Vring[32:64, vg, 0:4, :], vrow_g(0, 1536 + b_, 16, 32, 4, 4, p_)))
                        dmas.append((Vring[64:128, vg, 0:4, :], vrow_ap(pub_d, 0, b_, 16, 64, 4, 4, p_)))
                        dmas.append((Vring[0:64, vg, 4:8, :], vrow_ap(pub_d, 0, 16 * 64 + b_, 16, 64, 4, 4, p_)))
                        dmas.append((Vring[64:96, vg, 4:8, :], vrow_g(1, b_, 16, 32, 4, 4, p_)))
                        dmas.append((Vring[96:128, vg, 4:8, :], vrow_g(1, 512 + b_, 16, 32, 4, 4, p_)))
                        qsrc = lambda qb, c=c, kb=kb: Qp[:, kb, 1, 512 * c + 128 * qb:512 * c + 128 * qb + 128]
                        qres = qres16
                        blocks = [(a, a, 4 + a) for a in range(4)]
                    vall = [(vres, k_) for k_ in range(6)]

                    vgcnt[0] += 1

                    def pre(dmas=dmas, tiles=tiles, vg=vg, vres=vres, vall=vall, bid=vgcnt[0]):
                        if DBG_SKIP_V:
                            return
                        for k_, (dst, src) in enumerate(dmas):
                            S.add("sp", lambda e, dst=dst, src=src: e.dma_start(out=dst, in_=src),
                                  reads=gathV_all + pubV_all,
                                  writes=[(vres, k_)], dma=f"vg{vg}", batch=bid)

                        def f_vmask(e):
                            ins = None
                            for s_, (ks, edge) in enumerate(tiles):
                                if edge == 'L':
                                    ins = e.tensor_scalar(Vring[:, vg, s_, :], Vring[:, vg, s_, :], flags[:, 2:3], None, ALU.mult)
                                elif edge == 'R':
                                    ins = e.tensor_scalar(Vring[:, vg, s_, :], Vring[:, vg, s_, :], flags[:, 3:4], None, ALU.mult)
                            return ins
                    ab = acc_flip.next()
                    bO, bD = (4, 5) if ab == 0 else (6, 7)
                    for hf in range(2):
                        unit = dict(ci=ci, c=c, hf=hf, blks=blocks[2 * hf:2 * hf + 2], tiles=tiles, vg=vg, vres=vall, qsrc=qsrc, qres=qres,
                                    kb=kb, kres=kres, bO=bO, bD=bD, p=p_, last=(hf == 1), d=d, pre=(pre if hf == 0 else None))
                        pending.append(unit)

            def emit_qk(u):
                sbk = sbank_ring.next()
                u["sb"] = sbk

                def f(e, u=u, sbk=sbk):
                    ins = None
                    for bl, (qb, s0, s1) in enumerate(u["blks"]):
                        q = u["qsrc"](qb)
                        for hh in range(2):
                            ins = e.matmul(ps[2 * sbk + hh][:, bl * 256:bl * 256 + 256], ident_bf[:, :],
                                           Wt[:, u["ci"], u["p"], hh * 256:(hh + 1) * 256], start=True, stop=False)
                        for si, s_ in enumerate((s0, s1)):
                            ks = u["tiles"][s_][0]
                            for hh in range(2):
                                rows = slice(hh * 64, hh * 64 + 64)
                                c0_ = bl * 256 + si * 128
                                ins = e.matmul(ps[2 * sbk + hh][:, c0_:c0_ + 128],
                                               Kp[rows, u["kb"], ks], q[rows, :], start=False, stop=True,
                                               tile_position=(hh * 64, 0))
                        if u["ci"] != 2:
                            for si, s_ in enumerate((s0, s1)):
                                edge = u["tiles"][s_][1]
                                if edge:
                                    mo = 0 if edge == 'L' else 128
                                    for hh in range(2):
                                        c0_ = bl * 256 + si * 128
                                        ins = e.matmul(ps[2 * sbk + hh][:, c0_:c0_ + 128], mrow_bf[0:1, mo:mo + 128],
                                                       ones_bf[0:1, 0:128], start=False, stop=True)
                    return ins
                S.add("pe", f, reads=u["kres"] + u["qres"] + ["ident", "mrow", "ones"] + [("Wt", u["ci"], u["p"], hh) for hh in range(2)],
                      writes=[r_ps(2 * sbk), r_ps(2 * sbk + 1)])

            def emit_rest(u):
                sbk = u["sb"]
                pi = u["pi"] = (emit_rest.cnt % 2)
                emit_rest.cnt += 1
                def f_exp(e, sbk=sbk, pi=pi):
                    e.activation(Pw[:, pi, 0:512], ps[2 * sbk][:, :], AF.Exp)
                    return e.activation(Pw[:, pi, 512:1024], ps[2 * sbk + 1][:, :], AF.Exp)
                S.add("act", f_exp, reads=[r_ps(2 * sbk), r_ps(2 * sbk + 1)], writes=[("Pw", pi)])

                def f_pv(e, u=u, pi=pi):
                    ins = None
                    for bl, (qb, s0, s1) in enumerate(u["blks"]):
                        bi = 2 * u["hf"] + bl
                        cols = slice(bi * 128, bi * 128 + 128)
                        for si, s_ in enumerate((s0, s1)):
                            for hh in range(2):
                                rows = slice(hh * 64, hh * 64 + 64)
                                o_ = hh * 512 + bl * 256 + si * 128
                                pm = Pw[:, pi, o_:o_ + 128]
                                e.matmul(ps[u["bO"]][rows, cols], Vring[:, u["vg"], s_, hh * 64:hh * 64 + 64], pm,
                                         start=(si == 0), stop=(si == 1), tile_position=(0, hh * 64))
                                ins = e.matmul(ps[u["bD"]][rows, cols], ones_bf[:, 0:64], pm,
                                               start=(si == 0), stop=(si == 1), tile_position=(0, hh * 64))
                    return ins
                if not DBG_SKIP_PV:
                    S.add("pe", f_pv, reads=[("Pw", pi), "ones"] + u["vres"], writes=[r_ps(u["bO"]), r_ps(u["bD"])])
                if u["last"] and not DBG_SKIP_PV:
                    ci, c = u["ci"], u["c"]
                    if ci == 0:
                        vo = accO[:, 512 * c:512 * (c + 1)]
                        vd = accD[:, 512 * c:512 * (c + 1)]
                        S.add("dve", lambda e, u=u, vo=vo: e.tensor_copy(vo, ps[u["bO"]][:, :]),
                              reads=[r_ps(u["bO"])], writes=[r_big(c)])
                        S.add("act", lambda e, u=u, vd=vd: e.activation(vd, ps[u["bD"]][:, :], AF.Copy),
                              reads=[r_ps(u["bD"])], writes=[r_big(4 + c)])
                    else:
                        if ci == 1:
                            vo = accO.rearrange("p (j b) -> p b j", b=4)[:, c, :]
                            vd = accD.rearrange("p (j b) -> p b j", b=4)[:, c, :]
                            pso = lambda bk: ps[bk][:, :]
                        else:
                            vo = accO.rearrange("p (i a b) -> p b a i", a=4, b=4)[:, c, :, :]
                            vd = accD.rearrange("p (i a b) -> p b a i", a=4, b=4)[:, c, :, :]
                            pso = lambda bk: ps[bk][:, :].rearrange("p (a i) -> p a i", a=4)
                        S.add("dve", lambda e, u=u, vo=vo, pso=pso: e.tensor_tensor(vo, pso(u["bO"]), vo, ALU.add),
                              reads=[r_ps(u["bO"])] + [r_big(k) for k in range(4)], writes=[r_big(k) for k in range(4)])
                        t1 = tmp_ring.next()
                        S.add("act", lambda e, u=u, t1=t1: e.activation(TMP[:, t1, :], ps[u["bD"]][:, :], AF.Copy),
                              reads=[r_ps(u["bD"])], writes=[r_tmp(t1)])
                        tsrc = TMP[:, t1, :] if ci == 1 else TMP[:, t1, :].rearrange("p (a i) -> p a i", a=4)
                        S.add("dve", lambda e, vd=vd, tsrc=tsrc: e.tensor_tensor(vd, tsrc, vd, ALU.add),
                              reads=[r_tmp(t1)] + [r_big(4 + k) for k in range(4)], writes=[r_big(4 + k) for k in range(4)])
            emit_rest.cnt = 0

            LOOK = 1
            if DBG_SKIP_UNITS:
                pending = []
            if pending and pending[0]["pre"]:
                pending[0]["pre"]()
            if len(pending) > 2 and pending[2]["pre"]:
                pending[2]["pre"]()
            for i in range(len(pending) + LOOK):
                if i < len(pending):
                    emit_qk(pending[i])
                if i >= LOOK:
                    emit_rest(pending[i - LOOK])
                if i + 4 < len(pending) and pending[i + 4]["pre"]:
                    pending[i + 4]["pre"]()
                if i == len(pending) // 2 and p_ + 1 < 4:
                    prep_pair(p_ + 1)

            for c in range(4):
                t1 = tmp_ring.next()
                cs = slice(512 * c, 512 * (c + 1))

                S.add("act", lambda e, t1=t1, cs=cs: e.activation(TMP[:, t1, :], accD[:, cs], AF.Ln),
                      reads=[r_big(4 + c)], writes=[r_tmp(t1)])
                S.add("act", lambda e, t1=t1: e.activation(TMP[:, t1, :], TMP[:, t1, :], AF.Exp, scale=-1.0),
                      reads=[r_tmp(t1)], writes=[r_tmp(t1)])
                S.add("dve", lambda e, t1=t1, cs=cs: e.tensor_tensor(TMP[:, t1, :], accO[:, cs], TMP[:, t1, :], ALU.mult),
                      reads=[r_big(c), r_tmp(t1)], writes=[r_tmp(t1)])
                S.add("dve", lambda e, t1=t1, cs=cs, p_=p_: e.tensor_tensor(mix[:, p_, cs], TMP[:, t1, :], mix[:, p_, cs], ALU.mult),
                      reads=[r_tmp(t1), r_mix(p_, c)], writes=[r_mix(p_, c), r_big(c), r_big(4 + c)])

        checkpoint("attn")
        fix_reads = ["cnb", "cw", "flags"] + [("cub", f) for f in range(4)] + [("Bg", f, tt) for f in range(4) for tt in (0, 3)]

        def f_fix1(e, l=l):
            w0 = cwT[:, l, 0, :]; w1 = cwT[:, l, 1, :]; w2 = cwT[:, l, 2, :]
            e.tensor_scalar(fx[:, 0, :], cnb[:, 0, 3::4], flags[:, 0:1], None, ALU.mult)
            e.tensor_scalar(fx[:, 1, :], cnb[:, 1, 0::4], flags[:, 1:2], None, ALU.mult)
            e.tensor_tensor(fx[:, 2, :], cub[:, :, 0], w1, ALU.mult)
            e.tensor_tensor(fx[:, 3, :], cub[:, :, 1], w2, ALU.mult)
            e.tensor_tensor(fx[:, 4, :], cub[:, :, 3], w1, ALU.mult)
            return e.tensor_tensor(fx[:, 5, :], cub[:, :, 2], w0, ALU.mult)
        S.add("dve", f_fix1, reads=fix_reads, writes=["fx1"], sync_same=True)

        def f_fix2(e, l=l):
            w0 = cwT[:, l, 0, :]; w2 = cwT[:, l, 2, :]
            e.tensor_tensor(fx[:, 6, :], fx[:, 0, :], w0, ALU.mult)
            e.tensor_tensor(fx[:, 7, :], fx[:, 1, :], w2, ALU.mult)
            e.tensor_tensor(fx[:, 8, :], fx[:, 2, :], fx[:, 3, :], ALU.add)
            return e.tensor_tensor(fx[:, 9, :], fx[:, 4, :], fx[:, 5, :], ALU.add)
        S.add("dve", f_fix2, reads=["fx1", "cw"], writes=["fx2"], sync_same=True)

        def f_fix3(e):
            e.tensor_tensor(fx[:, 0, :], fx[:, 6, :], fx[:, 8, :], ALU.add)
            return e.tensor_tensor(fx[:, 1, :], fx[:, 7, :], fx[:, 9, :], ALU.add)
        S.add("dve", f_fix3, reads=["fx2", "fx1"], writes=["fx3"], sync_same=True)

        def f_fix4(e):
            e.tensor_tensor(fx[:, 2, :], fx[:, 0, :], Bg[:, :, 0], ALU.mult)
            return e.tensor_tensor(fx[:, 3, :], fx[:, 1, :], Bg[:, :, 1], ALU.mult)
        S.add("dve", f_fix4, reads=["fx3", "fx2", "fx1"] + fix_reads, writes=["fx4"], sync_same=True)

        def f_fix5(e):
            e.tensor_copy(mix[:, 4:8, 0], fx[:, 2, :])
            return e.tensor_copy(mix[:, 4:8, T - 1], fx[:, 3, :])
        S.add("dve", f_fix5, reads=["fx4"] + [r_mix(4 + f, tt) for f in range(4) for tt in (0, 3)],
              writes=[r_mix(4 + f, tt) for f in range(4) for tt in (0, 3)] + ["fx1"], sync_same=True)
        if STOP == "mixout":
            for ft in range(NFT):
                for tt in range(NTT):
                    S.add("dve", lambda e, ft=ft, tt=tt: e.tensor_copy(xT[:, ft, tt * TT:(tt + 1) * TT], mix[:, ft, tt * TT:(tt + 1) * TT]),
                          reads=[r_mix(ft, tt)], writes=[r_x(ft, tt)])
            raise _StopBuild()
        checkpoint("fix")
        wo = R1[:, 0:8192].rearrange("p (k c) -> p k c", k=8)
        S.add("pool", lambda e, l=l: e.dma_start(out=wo, in_=wout_d.ap()[l].rearrange("(ko ki) c -> ki ko c", ki=P)),
              writes=r_r1chunk(0) + r_r1chunk(1) + r_r1chunk(2) + r_r1chunk(3), dma="wo")
        wo_res = r_r1chunk(0) + r_r1chunk(1) + r_r1chunk(2) + r_r1chunk(3)
        o_sb = BIG[:].rearrange("p (c t) -> p c t", c=8)
        nxt = ada_tasks(l + 1) if l + 1 < NL else []
        for tt in range(NTT):
            tsl = slice(tt * TT, (tt + 1) * TT)
            for ct in range(8):
                if nxt:
                    nxt.pop(0)()
                bank = bank_ring.next()

                def f_o(e, ct=ct, tsl=tsl, bank=bank):
                    ins = None
                    for ko in range(8):
                        ins = e.matmul(ps[bank][:, :], wo[:, ko, ct * P:(ct + 1) * P], mix[:, ko, tsl], start=(ko == 0), stop=(ko == 7))
                    return ins
                S.add("pe", f_o, reads=wo_res + [r_mix(ko, tt) for ko in range(8)], writes=[r_ps(bank)])
                if ev_flip.next() == 0:
                    S.add("act", lambda e, ct=ct, bank=bank: e.activation(o_sb[:, ct, :], ps[bank][:, :], AF.Copy),
                          reads=[r_ps(bank)], writes=[r_big(ct)])
                else:
                    S.add("dve", lambda e, ct=ct, bank=bank: e.tensor_copy(o_sb[:, ct, :], ps[bank][:, :]),
                          reads=[r_ps(bank)], writes=[r_big(ct)])
            ri = rms_stats(lambda ft: o_sb[:, ft, :], lambda ft: r_big(ft), tt, 6)
            for ct in range(8):
                ti = tmp_ring.next()

                S.add("dve", lambda e, ct=ct, ti=ti, ri=ri: e.tensor_tensor(TMP[:, ti, :], o_sb[:, ct, :], rstdT[:, ri, :], ALU.mult),
                      reads=[r_big(ct), ("rstd", ri)], writes=[r_tmp(ti)])
                S.add("dve", lambda e, ct=ct, tsl=tsl, ti=ti, l=l: e.scalar_tensor_tensor(xT[:, ct, tsl], TMP[:, ti, :], ggate[:, l, ct:ct + 1], xT[:, ct, tsl], ALU.mult, ALU.add),
                      reads=[r_tmp(ti), ("ggate", l), r_x(ct, tt)], writes=[r_x(ct, tt)])
        while nxt:
            nxt.pop(0)()

    try:
        for l in range(NL):
            layer_body(l)
    except _StopBuild:
        pass
    for ft in range(NFT):
        i = S.add("sp", lambda e, ft=ft: e.dma_start(out=yT_d.ap()[ft * P:(ft + 1) * P, :], in_=xT[:, ft, :]),
                  reads=[r_x(ft, tt) for tt in range(NTT)], writes=[("y", ft)], dma=f"y{ft}")
        out_dmas.append(i)
    S.emit(final_waits=out_dmas)
    return nc


_PROG = {}


def _prep_inputs(x, c, w_ada, b_ada, pre_norm_g, w_in, conv_w, rel_bias, w_out, post_norm_g, l0, NL):
    onehot = make_onehot()
    sl = slice(l0, l0 + NL)
    wada = np.ascontiguousarray(w_ada[sl], np.float32)
    win = np.ascontiguousarray(w_in[sl], np.float32)
    wout = np.ascontiguousarray(w_out[sl], np.float32)
    badaT = np.ascontiguousarray(b_ada[sl].reshape(NL, 24, P).transpose(2, 0, 1), np.float32)
    gpreT = np.ascontiguousarray(pre_norm_g[sl].reshape(NL, 8, P).transpose(2, 0, 1), np.float32)
    gpostT = np.ascontiguousarray(post_norm_g[sl].reshape(NL, 8, P).transpose(2, 0, 1), np.float32)
    cwT = np.ascontiguousarray(conv_w[sl].reshape(NL, 3, 4, P).transpose(3, 0, 1, 2), np.float32)
    in_maps = []
    for core in range(NCORES):
        b, half = core // 2, core % 2
        xs = x[b, half * T:(half + 1) * T, :]
        fl = np.zeros((P, 6), np.float32)
        vprev = 1.0 if half == 1 else 0.0
        vnext = 1.0 if half == 0 else 0.0
        fl[:, 0] = vprev
        fl[:, 1] = vnext
        fl[:, 2] = 1.0
        fl[0:64, 2] = vprev
        fl[:, 3] = 1.0
        fl[64:128, 3] = vnext
        fl[0:64, 4] = 0.0 if half == 1 else NEG
        fl[64:128, 5] = 0.0 if half == 0 else NEG
        in_maps.append({
            "xT": np.ascontiguousarray(xs.T, np.float32),
            "cT": np.ascontiguousarray(c[b].reshape(8, P).T, np.float32),
            "w_ada": wada, "b_adaT": badaT, "gpreT": gpreT, "gpostT": gpostT,
            "w_in": win, "w_out": wout, "cwT": cwT,
            "rel_bias": np.ascontiguousarray(rel_bias, np.float32),
            "onehot": onehot, "flags": fl,
            "ident": np.eye(P, dtype=np.float32),
            "mrow": np.concatenate([fl[:, 4], fl[:, 5]])[None, :].astype(np.float32),
        })
    return in_maps


def _run(inputs, l0, NL, x_override=None):
    if NL not in _PROG:
        _PROG[NL] = build_program(NL)
    nc = _PROG[NL]
    x = inputs["x"] if x_override is None else x_override
    in_maps = _prep_inputs(x, inputs["c"], inputs["w_ada"], inputs["b_ada"], inputs["pre_norm_g"], inputs["w_in"],
                           inputs["conv_w"], inputs["rel_bias"], inputs["w_out"], inputs["post_norm_g"], l0, NL)
    res = run_bass_kernel_spmd(nc, in_maps, core_ids=list(range(NCORES)))
    B = x.shape[0]
    out = np.empty_like(np.asarray(x, np.float32))
    for core in range(NCORES):
        b, half = core // 2, core % 2
        out[b, half * T:(half + 1) * T, :] = res.results[core]["yT"].T
    return out


LAYERS_PER_LAUNCH = 4


def kernel(**inputs):
    inputs = {k: np.asarray(v) for k, v in inputs.items()}
    depth = inputs["w_in"].shape[0]
    x = np.asarray(inputs["x"], np.float32)
    for l0 in range(0, depth, LAYERS_PER_LAUNCH):
        x = _run(inputs, l0, LAYERS_PER_LAUNCH, x_override=x)
    return x.astype(np.float32)
```
